# Optimizing a Trainium2 kernel written in Bass

```python
import math
import jax, jax.numpy as jnp
from jax import lax
import numpy as np

D_MODEL = 1024
BATCH = 4
SEQ = 4096
DEPTH = 2

GRID_W = 64
CTX_LEN = 256
RMS_EPS = 1e-6

SSD_HEAD_DIM = 64
SSD_WIDTH = D_MODEL
SSD_HEADS = SSD_WIDTH // SSD_HEAD_DIM
SSD_GROUPS = 2
SSD_STATE = 128
SSD_CONV = 3
SSD_CHUNK = 128
SSD_XBC = SSD_WIDTH + 2 * SSD_GROUPS * SSD_STATE

POOL_WINDOWS = (2, 4, 8, 16)
POOL_WIDTH = D_MODEL // 2
POOL_GROUP_DIM = POOL_WIDTH // len(POOL_WINDOWS)

NA_HEAD_DIM = 64
NA_WIDTH = D_MODEL // 2
NA_HEADS = NA_WIDTH // NA_HEAD_DIM
NA_KH = 8
NA_KW = 16

MIX_WIDTH = SSD_WIDTH + POOL_WIDTH + NA_WIDTH
IN_WIDTH = SSD_WIDTH + SSD_XBC + 2 * SSD_HEADS + POOL_WIDTH + 3 * NA_WIDTH

PEER_HEADS = 8
PEER_NKEYS = 128
PEER_EXPERTS = PEER_NKEYS * PEER_NKEYS
PEER_QDIM = 256
PEER_TOPK = 16
PEER_BLOCK = 128

kernel_name = 'hybrid_ssd_pool_natten_peer_dit'


def rmsnorm(x, g):
    x32 = x.astype(jnp.float32)
    ms = jnp.mean(x32 * x32, axis=-1, keepdims=True)
    return (x32 * lax.rsqrt(ms + RMS_EPS)).astype(x.dtype) * g


def project(h, shift, scale, g, w_in):
    u = rmsnorm(h, g) * (1 + scale) + shift
    p = u @ w_in
    sizes = (SSD_WIDTH, SSD_XBC, 2 * SSD_HEADS, POOL_WIDTH, NA_WIDTH, NA_WIDTH, NA_WIDTH)
    points = [int(v) for v in np.cumsum(sizes)[:-1]]
    return jnp.split(p, points, axis=-1)


def to_heads(t):
    return t.reshape(t.shape[:-1] + (NA_HEADS, NA_HEAD_DIM))


def dwconv(x, w, b):
    k, ch = w.shape
    y = lax.conv_general_dilated(x, w[:, None, :].astype(x.dtype), window_strides=(1,),
                                 padding=[(k // 2, k // 2)], dimension_numbers=('NWC', 'WIO', 'NWC'),
                                 feature_group_count=ch)
    return y + b


def segsum(a):
    t = a.shape[-1]
    cs = jnp.cumsum(a, axis=-1)
    diff = cs[..., :, None] - cs[..., None, :]
    mask = jnp.tril(jnp.ones((t, t), dtype=bool))
    return jnp.where(mask, diff, -jnp.inf)


def ssd_chunked(x, dt, a_head, bm, cm, init, with_y):
    b, l, h, p = x.shape
    n = bm.shape[-1]
    nc = l // SSD_CHUNK
    xd = (x * dt[..., None]).reshape(b, nc, SSD_CHUNK, h, p)
    a = jnp.moveaxis((dt * a_head).reshape(b, nc, SSD_CHUNK, h), -1, 1)
    bc = bm.reshape(b, nc, SSD_CHUNK, h, n)
    a_cs = jnp.cumsum(a, axis=-1)
    decay_to_end = jnp.exp(a_cs[..., -1:] - a_cs)
    chunk_states = jnp.einsum('bclhn,bhcl,bclhp->bchpn', bc, decay_to_end, xd)
    states = jnp.concatenate([init[:, None], chunk_states], axis=1)
    chunk_decay = jnp.exp(segsum(jnp.pad(a_cs[..., -1], ((0, 0), (0, 0), (1, 0)))))
    states = jnp.einsum('bhzc,bchpn->bzhpn', chunk_decay, states)
    final = states[:, -1]
    if not with_y:
        return None, final
    cc = cm.reshape(b, nc, SSD_CHUNK, h, n)
    lmat = jnp.exp(segsum(a))
    cb = jnp.einsum('bclhn,bcshn->bhcls', cc, bc)
    y_diag = jnp.einsum('bhcls,bcshp->bclhp', cb * lmat, xd)
    in_decay = jnp.exp(a_cs).transpose(0, 2, 3, 1)[..., None]
    y_off = jnp.einsum('bclhn,bchpn->bclhp', cc, states[:, :-1]) * in_decay
    return (y_diag + y_off).reshape(b, l, h, p), final


def ssd_mixer(z, xbc_raw, dt_raw, conv_w, conv_b, a_log, dt_bias, d_skip, norm_g, init_f, init_b, with_y):
    xbc = jax.nn.silu(dwconv(xbc_raw, conv_w, conv_b))
    b, l, _ = xbc.shape
    gs = SSD_GROUPS * SSD_STATE
    xs = xbc[..., :SSD_WIDTH].reshape(b, l, SSD_HEADS, SSD_HEAD_DIM).astype(jnp.float32)
    hpg = SSD_HEADS // SSD_GROUPS
    bm = jnp.repeat(xbc[..., SSD_WIDTH:SSD_WIDTH + gs].reshape(b, l, SSD_GROUPS, SSD_STATE), hpg, axis=2).astype(jnp.float32)
    cm = jnp.repeat(xbc[..., SSD_WIDTH + gs:].reshape(b, l, SSD_GROUPS, SSD_STATE), hpg, axis=2).astype(jnp.float32)
    dt = jax.nn.softplus(dt_raw.astype(jnp.float32).reshape(b, l, 2, SSD_HEADS) + dt_bias.astype(jnp.float32))
    a_head = -jnp.exp(a_log.astype(jnp.float32))
    if init_f is None:
        init_f = jnp.zeros((b, SSD_HEADS, SSD_HEAD_DIM, SSD_STATE), jnp.float32)
        init_b = init_f
    flip = lambda t: jnp.flip(t, axis=1)
    y_f, s_f = ssd_chunked(xs, dt[:, :, 0], a_head[0], bm, cm, init_f, with_y)
    y_b, s_b = ssd_chunked(flip(xs), flip(dt[:, :, 1]), a_head[1], flip(bm), flip(cm), init_b, with_y)
    if not with_y:
        return None, s_f, s_b
    y = y_f + flip(y_b) + d_skip.astype(jnp.float32)[:, None] * xs
    y = y.reshape(b, l, SSD_WIDTH) * jax.nn.silu(z.astype(jnp.float32))
    return rmsnorm(y, norm_g).astype(z.dtype), s_f, s_b


def pool_mixer(u, w_pool, scale):
    b, l, _ = u.shape
    t = jnp.arange(l)
    cs = jnp.pad(jnp.cumsum(u.astype(jnp.float32), axis=1), ((0, 0), (1, 0), (0, 0)))
    outs = []
    for gi, w in enumerate(POOL_WINDOWS):
        sl = slice(gi * POOL_GROUP_DIM, (gi + 1) * POOL_GROUP_DIM)
        lo = jnp.clip(t - w // 2, 0, l - 1)
        hi = jnp.clip(t + w - w // 2 - 1, 0, l - 1)
        seg = cs[..., sl]
        cnt = (hi - lo + 1).astype(jnp.float32)[:, None]
        mean = (jnp.take(seg, hi + 1, axis=1) - jnp.take(seg, lo, axis=1)) / cnt
        outs.append(mean - u[..., sl].astype(jnp.float32))
    pooled = jnp.stack(outs, axis=2)
    y = jnp.einsum('blgi,gio->blgo', pooled, w_pool).reshape(b, l, POOL_WIDTH)
    return (y * scale).astype(u.dtype)


def context_attention(q, k, v):
    b, l, h, d = q.shape
    s = jnp.einsum('bqhd,bkhd->bhqk', q * (d ** -0.5), k).astype(jnp.float32)
    p = jax.nn.softmax(s, axis=-1).astype(v.dtype)
    return jnp.einsum('bhqk,bkhd->bqhd', p, v).reshape(b, l, h * d)


def neighbourhood_attention(q, k, v, k_ctx, v_ctx, rpb):
    b, s, h, d = q.shape
    rows = s // GRID_W
    kh = min(NA_KH, rows)
    qg = (q * (d ** -0.5)).reshape(b, rows, GRID_W, h, d)
    kg = k.reshape(b, rows, GRID_W, h, d)
    vg = v.reshape(b, rows, GRID_W, h, d)
    row_start = jnp.clip(jnp.arange(rows) - kh // 2, 0, rows - kh)
    col = jnp.arange(GRID_W)
    col_start = jnp.clip(col - NA_KW // 2, 0, GRID_W - NA_KW)
    in_window = (col[None, :] >= col_start[:, None]) & (col[None, :] < col_start[:, None] + NA_KW)
    dc = jnp.clip(col[None, :] - col[:, None], -(NA_KW - 1), NA_KW - 1) + NA_KW - 1
    col_bias = jnp.where(in_window[None, None], rpb[:, :, dc].astype(jnp.float32), -jnp.inf)

    def row_block(args):
        q_row, r0, r = args
        k_blk = lax.dynamic_slice_in_dim(kg, r0, kh, axis=1)
        v_blk = lax.dynamic_slice_in_dim(vg, r0, kh, axis=1)
        dr = r0 + jnp.arange(kh) - r + NA_KH - 1
        bias = jnp.take(col_bias, dr, axis=1).transpose(0, 2, 1, 3)
        s_loc = jnp.einsum('bqhd,bkwhd->bhqkw', q_row, k_blk).astype(jnp.float32) + bias[None]
        s_ctx = jnp.einsum('bqhd,bchd->bhqc', q_row, k_ctx).astype(jnp.float32)
        s_all = jnp.concatenate([s_loc.reshape(b, h, GRID_W, kh * GRID_W), s_ctx], axis=-1)
        p = jax.nn.softmax(s_all, axis=-1).astype(v.dtype)
        p_loc = p[..., :kh * GRID_W].reshape(b, h, GRID_W, kh, GRID_W)
        return (jnp.einsum('bhqkw,bkwhd->bqhd', p_loc, v_blk)
                + jnp.einsum('bhqc,bchd->bqhd', p[..., kh * GRID_W:], v_ctx))

    out = lax.map(row_block, (jnp.moveaxis(qg, 1, 0), row_start, jnp.arange(rows)))
    return jnp.moveaxis(out, 0, 1).reshape(b, s, h * d)


def peer_ffn(u, w_q, sub_keys, u_tab, v_tab):
    b, l, dm = u.shape
    t = u.reshape(b * l, dm)
    n_tok = b * l
    q = (t @ w_q).reshape(n_tok, PEER_HEADS, 2, PEER_QDIM // 2)
    s = jnp.einsum('thid,hikd->thik', q, sub_keys).astype(jnp.float32)
    v1, i1 = lax.top_k(s[:, :, 0], PEER_TOPK)
    v2, i2 = lax.top_k(s[:, :, 1], PEER_TOPK)
    cand = (v1[..., :, None] + v2[..., None, :]).reshape(n_tok, PEER_HEADS, PEER_TOPK * PEER_TOPK)
    cand_idx = (i1[..., :, None] * PEER_NKEYS + i2[..., None, :]).reshape(n_tok, PEER_HEADS, PEER_TOPK * PEER_TOPK)
    top_s, pos = lax.top_k(cand, PEER_TOPK)
    idx = jnp.take_along_axis(cand_idx, pos, axis=-1).reshape(n_tok, PEER_HEADS * PEER_TOPK)
    g = jax.nn.softmax(top_s, axis=-1).astype(u.dtype).reshape(n_tok, PEER_HEADS * PEER_TOPK)
    nb = n_tok // PEER_BLOCK

    def block(args):
        xb, ib, gb = args
        act = jax.nn.gelu(jnp.einsum('tkd,td->tk', jnp.take(u_tab, ib, axis=0), xb), approximate=False)
        return jnp.einsum('tk,tkd->td', gb * act, jnp.take(v_tab, ib, axis=0))

    out = lax.map(block, (t.reshape(nb, PEER_BLOCK, dm), idx.reshape(nb, PEER_BLOCK, -1), g.reshape(nb, PEER_BLOCK, -1)))
    return out.reshape(b, l, dm)


def setup_inputs(seed: int = 0) -> dict:
    key = jax.random.key(seed)
    ks = jax.random.split(key, 24)
    L, D = DEPTH, D_MODEL
    nrm = lambda k, shape, s: jax.random.normal(k, shape, jnp.float32) * s
    dt0 = jnp.exp(jax.random.uniform(ks[9], (L, 2, SSD_HEADS), jnp.float32, math.log(1e-3), math.log(1e-1)))
    dt_bias = dt0 + jnp.log(-jnp.expm1(-dt0))
    return {
        'x': nrm(ks[0], (BATCH, SEQ, D), 1.0),
        'c': nrm(ks[1], (BATCH, D), 1.0),
        'ctx': nrm(ks[2], (BATCH, CTX_LEN, D), 1.0),
        'c_ctx': nrm(ks[3], (D,), 0.5),
        'ada_w': nrm(ks[4], (L, D, 6 * D), 0.5 * D ** -0.5),
        'ada_b': nrm(ks[5], (L, 6 * D), 0.01),
        'norm1_g': 1.0 + nrm(ks[6], (L, D), 0.01),
        'w_in': nrm(ks[7], (L, D, IN_WIDTH), D ** -0.5),
        'conv_w': nrm(ks[8], (L, SSD_CONV, SSD_XBC), SSD_CONV ** -0.5),
        'conv_b': nrm(ks[10], (L, SSD_XBC), 0.01),
        'a_log': jnp.log(jax.random.uniform(ks[11], (L, 2, SSD_HEADS), jnp.float32, 1.0, 16.0)),
        'dt_bias': dt_bias,
        'd_skip': 1.0 + nrm(ks[12], (L, SSD_HEADS), 0.1),
        'ssd_norm_g': 1.0 + nrm(ks[13], (L, SSD_WIDTH), 0.01),
        'pool_w': nrm(ks[14], (L, len(POOL_WINDOWS), POOL_GROUP_DIM, POOL_GROUP_DIM), POOL_GROUP_DIM ** -0.5),
        'pool_scale': 1.0 + nrm(ks[15], (L, POOL_WIDTH), 0.1),
        'na_rpb': nrm(ks[16], (L, NA_HEADS, 2 * NA_KH - 1, 2 * NA_KW - 1), 0.02),
        'w_out': nrm(ks[17], (L, MIX_WIDTH, D), MIX_WIDTH ** -0.5),
        'norm2_g': 1.0 + nrm(ks[18], (L, D), 0.01),
        'peer_wq': nrm(ks[19], (L, D, PEER_HEADS * PEER_QDIM), D ** -0.5),
        'peer_keys': nrm(ks[20], (L, PEER_HEADS, 2, PEER_NKEYS, PEER_QDIM // 2), (PEER_QDIM // 2) ** -0.5),
        'peer_u': nrm(ks[21], (L, PEER_EXPERTS, D), D ** -0.5),
        'peer_v': nrm(ks[22], (L, PEER_EXPERTS, D), PEER_TOPK ** -0.5),
        'final_g': 1.0 + nrm(ks[23], (D,), 0.01),
    }


def reference(x, c, ctx, c_ctx, ada_w, ada_b, norm1_g, w_in, conv_w, conv_b, a_log, dt_bias, d_skip,
              ssd_norm_g, pool_w, pool_scale, na_rpb, w_out, norm2_g, peer_wq, peer_keys, peer_u, peer_v, final_g):
    h, hc = x, ctx
    for i in range(DEPTH):
        need_ctx_out = i < DEPTH - 1
        mod = jax.nn.silu(c) @ ada_w[i] + ada_b[i]
        sh1, sc1, g1, sh2, sc2, g2 = jnp.split(mod[:, None, :], 6, axis=-1)
        mod_c = jax.nn.silu(c_ctx) @ ada_w[i] + ada_b[i]
        csh1, csc1, cg1, csh2, csc2, cg2 = jnp.split(mod_c, 6, axis=-1)
        ssd_args = (conv_w[i], conv_b[i], a_log[i], dt_bias[i], d_skip[i], ssd_norm_g[i])

        zc, xbcc, dtc, poolc, qc, kc, vc = project(hc, csh1, csc1, norm1_g[i], w_in[i])
        kc, vc = to_heads(kc), to_heads(vc)
        ssd_c, st_f, st_b = ssd_mixer(zc, xbcc, dtc, *ssd_args, None, None, need_ctx_out)
        if need_ctx_out:
            mix_c = jnp.concatenate([ssd_c, pool_mixer(poolc, pool_w[i], pool_scale[i]),
                                     context_attention(to_heads(qc), kc, vc)], axis=-1) @ w_out[i]
            hc = hc + cg1 * mix_c
            uc = rmsnorm(hc, norm2_g[i]) * (1 + csc2) + csh2
            hc = hc + cg2 * peer_ffn(uc, peer_wq[i], peer_keys[i], peer_u[i], peer_v[i])

        z, xbc, dtr, pl, q, k, v = project(h, sh1, sc1, norm1_g[i], w_in[i])
        ssd_l, _, _ = ssd_mixer(z, xbc, dtr, *ssd_args, st_f, st_b, True)
        na = neighbourhood_attention(to_heads(q), to_heads(k), to_heads(v), kc, vc, na_rpb[i])
        mix = jnp.concatenate([ssd_l, pool_mixer(pl, pool_w[i], pool_scale[i]), na], axis=-1) @ w_out[i]
        h = h + g1 * mix
        u = rmsnorm(h, norm2_g[i]) * (1 + sc2) + sh2
        h = h + g2 * peer_ffn(u, peer_wq[i], peer_keys[i], peer_u[i], peer_v[i])
    return rmsnorm(h, final_g)
```

```python
import numpy as np
from contextlib import ExitStack, contextmanager
import concourse.bass as bass
import concourse.mybir as mybir
from concourse.bass_utils import run_bass_kernel_spmd

F32 = mybir.dt.float32
BF16 = mybir.dt.bfloat16
U32 = mybir.dt.uint32
I32 = mybir.dt.int32
AF = mybir.ActivationFunctionType
ALU = mybir.AluOpType

EPOCH = 30000
NSEM_PER_ENG = 14
NDMA_SEM = 40


class Buf:
    __slots__ = ("name", "lw", "rd")

    def __init__(self, name=""):
        self.name = name
        self.lw = None
        self.rd = {}


class Prog:
    def __init__(self, nc, stack):
        self.nc = nc
        self.stacks = [stack]
        self.engs = {"pe": nc.tensor, "dve": nc.vector, "act": nc.scalar,
                     "pool": nc.gpsimd, "sp": nc.sync}
        self.cnt = {e: 0 for e in self.engs}
        self.sems = {e: [] for e in self.engs}
        self.dma_sems = [stack.enter_context(nc.semaphore(f"s_dma_{i}")) for i in range(NDMA_SEM)]
        self.dma_val = [0] * NDMA_SEM
        self.dma_next = 0
        self.known = {e: {f: 0 for f in self.engs} for e in self.engs}
        self.known_dma = {e: [0] * NDMA_SEM for e in self.engs}
        self.nwaits = 0
        self.ninstr = 0
        self.uid = 0

    def esem(self, e, ep):
        while len(self.sems[e]) <= ep:
            self.sems[e].append(self.stacks[0].enter_context(self.nc.semaphore(f"s_{e}_{len(self.sems[e])}")))
        return self.sems[e][ep]

    def sb(self, name, shape, dt=F32):
        self.uid += 1
        t = self.stacks[-1].enter_context(self.nc.sbuf_tensor(f"{name}_{self.uid}", list(shape), dt))
        return t, Buf(name)

    def ps(self, name, shape, dt=F32):
        self.uid += 1
        t = self.stacks[-1].enter_context(self.nc.psum_tensor(f"{name}_{self.uid}", list(shape), dt))
        return t, Buf(name)

    @contextmanager
    def scope(self):
        st = ExitStack()
        self.stacks.append(st)
        try:
            yield
        finally:
            self.barrier()
            self.stacks.pop()
            st.close()

    def _wait_tok(self, e, tok):
        if tok is None:
            return
        eng = self.engs[e]
        if tok[0] == "e":
            _, f, idx = tok
            if f == e and e == "pe":
                return
            if self.known[e][f] >= idx:
                return
            ep = (idx - 1) // EPOCH
            eng.wait_ge(self.esem(f, ep), (idx - 1) % EPOCH + 1)
            self.known[e][f] = idx
            self.nwaits += 1
        else:
            _, s, val = tok
            if self.known_dma[e][s] >= val:
                return
            eng.wait_ge(self.dma_sems[s], val)
            self.known_dma[e][s] = val
            self.nwaits += 1

    def _deps(self, e, reads, writes):
        for b in reads:
            self._wait_tok(e, b.lw)
        for b in writes:
            self._wait_tok(e, b.lw)
            for tok in b.rd.values():
                self._wait_tok(e, tok)

    def _mark(self, tok, reads, writes):
        key = tok[1] if tok[0] == "e" else ("d", tok[1])
        for b in reads:
            b.rd[key] = tok
        for b in writes:
            b.lw = tok
            b.rd = {}

    def op(self, e, fn, reads=(), writes=()):
        self._deps(e, reads, writes)
        ins = fn(self.engs[e])
        self.cnt[e] += 1
        idx = self.cnt[e]
        ep = (idx - 1) // EPOCH
        assert ep < NSEM_PER_ENG, "instruction budget exceeded on " + e
        ins.then_inc(self.esem(e, ep), 1)
        self._mark(("e", e, idx), reads, writes)
        self.ninstr += 1
        return ins

    def _dma_common(self, q, reads, writes, issue):
        self._deps(q, reads, writes)
        s = self.dma_next
        self.dma_next = (self.dma_next + 1) % NDMA_SEM
        if self.dma_val[s] > 0:
            self._wait_tok(q, ("d", s, self.dma_val[s]))
        self.dma_val[s] += 16
        ins = issue(self.engs[q])
        ins.then_inc(self.dma_sems[s], 16)
        self._mark(("d", s, self.dma_val[s]), reads, writes)
        self.ninstr += 1
        return ins

    def dma(self, q, out, in_, reads=(), writes=(), **kw):
        return self._dma_common(q, reads, writes, lambda e: e.dma_start(out=out, in_=in_, **kw))

    def idma(self, out, in_, off_ap, reads=(), writes=()):
        return self._dma_common(
            "pool", reads, writes,
            lambda e: e.indirect_dma_start(out=out, out_offset=None, in_=in_,
                                           in_offset=bass.IndirectOffsetOnAxis(ap=off_ap, axis=0)))

    def barrier(self):
        for e in self.engs:
            for f in self.engs:
                if f != e and self.cnt[f] > 0:
                    self._wait_tok(e, ("e", f, self.cnt[f]))
            for s in range(NDMA_SEM):
                if self.dma_val[s] > 0:
                    self._wait_tok(e, ("d", s, self.dma_val[s]))

    def mm(self, out, lhsT, rhs, start, stop, r, w):
        return self.op("pe", lambda e: e.matmul(out=out, lhsT=lhsT, rhs=rhs, start=start, stop=stop), r, w)

    def tr(self, out, in_, ident, r, w):
        return self.op("pe", lambda e: e.transpose(out=out, in_=in_, identity=ident), r, w)

    def tt(self, eng, out, in0, in1, op, r, w):
        return self.op(eng, lambda e: e.tensor_tensor(out=out, in0=in0, in1=in1, op=op), r, w)

    def ts(self, eng, out, in0, s1, op0, r, w, s2=None, op1=None):
        if op1 is None:
            return self.op(eng, lambda e: e.tensor_scalar(out=out, in0=in0, scalar1=s1, scalar2=None, op0=op0), r, w)
        return self.op(eng, lambda e: e.tensor_scalar(out=out, in0=in0, scalar1=s1, scalar2=s2, op0=op0, op1=op1), r, w)

    def stt(self, out, in0, scalar, in1, op0, op1, r, w, accum=None):
        return self.op("dve", lambda e: e.scalar_tensor_tensor(out=out, in0=in0, scalar=scalar, in1=in1,
                                                               op0=op0, op1=op1, accum_out=accum), r, w)

    def actv(self, out, in_, func, r, w, bias=None, scale=None, accum=None):
        kw = {}
        if bias is not None:
            kw["bias"] = bias
        if scale is not None:
            kw["scale"] = scale
        if accum is not None:
            kw["accum_out"] = accum
        return self.op("act", lambda e: e.activation(out=out, in_=in_, func=func, **kw), r, w)

    def cp(self, eng, out, in_, r, w):
        if eng == "act":
            return self.op("act", lambda e: e.copy(out=out, in_=in_), r, w)
        return self.op(eng, lambda e: e.tensor_copy(out=out, in_=in_), r, w)

    def memset(self, eng, ap, val, w):
        return self.op(eng, lambda e: e.memset(ap, val), (), w)


D = 1024
NT = 34
UTW = 4384
NEG = -30000.0
ZO, XO, DTO, PLO, QO, KO, VO = 0, 1024, 2560, 2592, 3104, 3616, 4128


def tcol(i):
    return 8 + i * 128 if i < 2 else 280 + (i - 2) * 128


def make_consts():
    k = np.arange(128)
    sec = {}
    cols = []
    off = [0]

    def add(name, arr):
        a = np.zeros((128, arr.shape[1]), np.float32)
        a[:arr.shape[0]] = arr
        sec[name] = (off[0], arr.shape[1])
        off[0] += arr.shape[1]
        cols.append(a)

    add("ident", np.eye(128))
    add("ones", np.ones((128, 128)))
    add("triF", (k[:, None] <= k[None, :]).astype(np.float32))
    add("triB", (k[:, None] >= k[None, :]).astype(np.float32))
    nmF = np.where(k[None, :] < k[:, None], NEG, 0.0)
    nmB = np.where(k[None, :] > k[:, None], NEG, 0.0)
    add("nmF", np.tile(nmF, (1, 4)))
    add("nmB", np.tile(nmB, (1, 4)))
    add("iota", np.tile(np.arange(256)[None, :], (128, 1)).astype(np.float32))
    q = np.arange(64)
    cs = np.clip(q - 8, 0, 48)
    kc = np.arange(64)
    valid = (kc[:, None] >= cs[None, :]) & (kc[:, None] < cs[None, :] + 16)
    add("cmask", np.where(valid, 0.0, NEG))
    add("J64", np.eye(64)[::-1].copy())
    MA = np.zeros((128, 3, 4, 128), np.float32)
    MB = np.zeros((16, 3, 4, 128), np.float32)
    t = np.arange(128)
    for v in range(3):
        for gi, w in enumerate((2, 4, 8, 16)):
            lo = t - w // 2
            hi = t + w - w // 2 - 1
            if v == 1:
                lo = np.maximum(lo, 0)
            if v == 2:
                hi = np.minimum(hi, 127)
            cnt = (hi - lo + 1).astype(np.float32)
            for part, M, s_abs in ((0, MA, np.arange(128) - 8), (1, MB, 120 + np.arange(16))):
                inw = (s_abs[:, None] >= lo[None, :]) & (s_abs[:, None] <= hi[None, :])
                M[:, v, gi, :] = inw / cnt[None, :] - (s_abs[:, None] == t[None, :])
    ncom = off[0]
    add("MA", MA.reshape(128, -1))
    add("MB", MB.reshape(16, -1))
    return np.concatenate(cols, axis=1), sec, ncom


CST, SEC, NCOM = make_consts()
NCST = CST.shape[1]

WSPEC = [("ada_w", [2, 1024, 6144]), ("ada_b", [2, 6144]), ("norm1_g", [2, 1024]), ("w_in", [2, 1024, 4640]),
         ("conv_w", [2, 3, 1536]), ("conv_b", [2, 1536]), ("a_log", [2, 2, 16]), ("dt_bias", [2, 2, 16]),
         ("d_skip", [2, 16]), ("ssd_norm_g", [2, 1024]), ("pool_w", [2, 4, 128, 128]), ("pool_scale", [2, 512]),
         ("na_rpb", [2, 8, 15, 31]), ("w_out", [2, 2048, 1024]), ("norm2_g", [2, 1024]),
         ("peer_wq", [2, 1024, 2048]), ("peer_keys", [2, 8, 2, 128, 128]), ("peer_u", [2, 16384, 1024]),
         ("peer_v", [2, 16384, 1024]), ("final_g", [1024])]


def build_program(nlayers=2, dbg=None):
    nc = bass.Bass("TRN2", target_bir_lowering=False)

    def din(name, shape, dt=F32):
        return nc.dram_tensor(name, list(shape), dt, kind="ExternalInput").ap()

    def dscr(name, shape, dt=F32):
        return nc.dram_tensor(name, list(shape), dt, kind="Internal").ap()

    x_in = din("x", [4096, D])
    ctx_in = din("ctx", [256, D])
    c2_in = din("c2", [2, D])
    cst_in = din("cst", [128, NCST])
    W = {n: din(n, s) for n, s in WSPEC}
    y_out = nc.dram_tensor("y", [4096, D], F32, kind="ExternalOutput").ap()
    dbg_out = {}
    if dbg:
        for n, s in dbg.items():
            if n.startswith("_"):
                continue
            dbg_out[n] = nc.dram_tensor(n, list(s), F32, kind="ExternalOutput").ap()

    hS = dscr("hS", [NT * 128, D])
    mixS = dscr("mixS", [NT * 128, 2048], BF16)
    stS = dscr("stS", [2, NT, 128, 1024], BF16)
    csbS = dscr("csbS", [NT, 128, 1024])
    KTs = dscr("KTs", [64, 8, 4096], BF16)
    Vs = dscr("Vs", [64, 64, 520], BF16)
    rpbP = dscr("rpbP", [120, 160])
    gS = dscr("gS", [128, 4096])
    gSb = Buf("gS")

    hB = [Buf(f"h{i}") for i in range(NT)]
    mixB = [Buf(f"mix{i}") for i in range(NT)]
    stB = [[Buf(f"st{d}_{i}") for i in range(NT)] for d in range(2)]
    csbB = [Buf(f"csb{i}") for i in range(NT)]
    ktB = [Buf(f"kt{i}") for i in range(32)]
    vB = [Buf(f"v{i}") for i in range(32)]
    rpbB = Buf("rpbP")
    yB = Buf("y")
    dbgB = Buf("dbg")

    with ExitStack() as st0:
        P = Prog(nc, st0)
        cst, cstb = P.sb("cst", [128, NCOM])
        P.dma("sp", cst[:], cst_in[:, 0:NCOM], writes=[cstb])

        def C(name, rows=128):
            o, w = SEC[name]
            return cst[0:rows, o:o + w]

        identf = C("ident")
        ones = C("ones")
        identb, identbb = P.sb("identb", [128, 128], BF16)
        P.cp("dve", identb[:], identf, [cstb], [identbb])
        zero, zerob = P.sb("zero", [128, 160])
        P.memset("pool", zero[:], 0.0, [zerob])

        for l in range(nlayers):
            last = l == nlayers - 1
            tiles_all = list(range(NT))
            out_tiles = list(range(2, NT)) if last else list(range(NT))
            with P.scope():
                n1g, n1gb = P.sb("n1g", [128, 8])
                n2g, n2gb = P.sb("n2g", [128, 8])
                adab, adabb = P.sb("adab", [128, 48])
                scT, scTb = P.sb("scT", [128, 8, 2])
                modT, modTb = P.sb("modT", [128, 48, 2])
                A1, A1b = P.sb("A1", [128, 8, 2])
                A2, A2b = P.sb("A2", [128, 8, 2])
                slow = dict(allow_slow_non_contiguous=True)
                P.dma("sp", n1g[:], W["norm1_g"][l].rearrange("(k p) -> p k", p=128), writes=[n1gb], **slow)
                P.dma("sp", n2g[:], W["norm2_g"][l].rearrange("(k p) -> p k", p=128), writes=[n2gb], **slow)
                P.dma("sp", adab[:], W["ada_b"][l].rearrange("(k p) -> p k", p=128), writes=[adabb], **slow)
                for r_ in range(2):
                    P.dma("sp", scT[:, :, r_], c2_in[r_].rearrange("(k p) -> p k", p=128), writes=[scTb], **slow)
                P.actv(scT[:], scT[:], AF.Silu, [scTb], [scTb])
                with P.scope():
                    gbc, gbcb = P.sb("gbc", [128, 2, 2, 1024])
                    scB, scBb = P.sb("scB", [128, 8, 2, 128])
                    P.cp("dve", scB[:], scT[:].unsqueeze(3).to_broadcast([128, 8, 2, 128]), [scTb], [scBb])
                    adr, adrb = P.sb("adr", [128, 2, 1024])
                    P.dma("sp", adr[:, 0, :], W["ada_b"][l, 2048:3072].partition_broadcast(128), writes=[adrb])
                    P.dma("sp", adr[:, 1, :], W["ada_b"][l, 5120:6144].partition_broadcast(128), writes=[adrb])
                    aw, awb = P.sb("aw", [128, 8, 512])
                    modps, modpsb = P.ps("modps", [128, 48, 2])
                    gps, gpsb = P.ps("gps", [128, 512])
                    for pc in range(12):
                        P.dma("sp", aw[:], W["ada_w"][l, :, pc * 512:(pc + 1) * 512].rearrange("(k p) n -> p k n", p=128),
                              writes=[awb])
                        for n4 in range(4):
                            j = pc * 4 + n4
                            for k in range(8):
                                P.mm(modps[:, j, :], aw[:, k, n4 * 128:(n4 + 1) * 128], scT[:, k, :], k == 0, k == 7,
                                     [awb, scTb], [modpsb])
                        if pc in (4, 5, 10, 11):
                            which = 0 if pc < 6 else 1
                            half = pc % 2
                            for r in range(2):
                                for k in range(8):
                                    P.mm(gps[:], scB[:, k, r, :], aw[:, k, :], k == 0, k == 7, [scBb, awb], [gpsb])
                                P.tt("dve", gbc[:, which, r, half * 512:(half + 1) * 512], gps[:],
                                     adr[:, which, half * 512:(half + 1) * 512], ALU.add, [gpsb, adrb], [gbcb])
                    P.dma("sp", gS[:, :], gbc[:].rearrange("p a b c -> p (a b c)"), reads=[gbcb], writes=[gSb])
                    P.tt("dve", modT[:], modps[:], adab[:].unsqueeze(2).to_broadcast([128, 48, 2]), ALU.add,
                         [modpsb, adabb], [modTb])
                    P.ts("dve", A1[:], modT[:, 8:16, :], 1.0, ALU.add, [modTb], [A1b])
                    P.tt("dve", A1[:], A1[:], n1g[:].unsqueeze(2).to_broadcast([128, 8, 2]), ALU.mult, [A1b, n1gb], [A1b])
                    P.ts("dve", A2[:], modT[:, 32:40, :], 1.0, ALU.add, [modTb], [A2b])
                    P.tt("dve", A2[:], A2[:], n2g[:].unsqueeze(2).to_broadcast([128, 8, 2]), ALU.mult, [A2b, n2gb], [A2b])
                B1 = modT[:, 0:8, :]
                B2 = modT[:, 24:32, :]
                mix_scope = P.scope()
                mix_scope.__enter__()
                uT, uTb = P.sb("uT", [128, 8, UTW], BF16)
                P.memset("pool", uT[:], 0.0, [uTb])
                uTa = uTd = uTb

                with P.scope():
                    ht, htb = P.sb("ht", [128, D])
                    junk, junkb = P.sb("junk", [128, D])
                    ss, ssb = P.sb("ss", [128, 1])
                    xn, xnb = P.sb("xn", [128, D], BF16)
                    tp8, tp8b = P.ps("tp8", [128, 8, 128], BF16)
                    for i in tiles_all:
                        if l == 0:
                            src = ctx_in[i * 128:(i + 1) * 128, :] if i < 2 else x_in[(i - 2) * 128:(i - 1) * 128, :]
                            rds = []
                        else:
                            src = hS[i * 128:(i + 1) * 128, :]
                            rds = [hB[i]]
                        r = 1 if i < 2 else 0
                        col = tcol(i)
                        P.dma("sp", ht[:], src, reads=rds, writes=[htb])
                        P.actv(junk[:], ht[:], AF.Square, [htb], [junkb, ssb], accum=ss[:])
                        P.actv(ss[:], ss[:], AF.Sqrt, [ssb], [ssb], bias=1e-6, scale=1.0 / D)
                        P.op("dve", lambda e: e.reciprocal(out=ss[:], in_=ss[:]), [ssb], [ssb])
                        P.ts("dve", xn[:], ht[:], ss[:, 0:1], ALU.mult, [htb, ssb], [xnb])
                        for k in range(8):
                            P.tr(tp8[:, k, :], xn[:, k * 128:(k + 1) * 128], identb[:], [xnb, identbb], [tp8b])
                        for k in range(8):
                            if k < 4:
                                P.actv(uT[:, k, col:col + 128], tp8[:, k, :], AF.Identity, [tp8b, A1b, modTb], [uTa],
                                       bias=B1[:, k, r:r + 1], scale=A1[:, k, r:r + 1])
                            else:
                                P.ts("dve", uT[:, k, col:col + 128], tp8[:, k, :], A1[:, k, r:r + 1], ALU.mult,
                                     [tp8b, A1b, modTb], [uTd], s2=B1[:, k, r:r + 1], op1=ALU.add)

                uTr = [uTb]
                with P.scope():
                    winA, winAb = P.sb("winA", [128, 8, 2592], BF16)
                    for k in range(8):
                        for c0 in range(0, 2592, 864):
                            P.dma("pool", winA[:, k, c0:c0 + 864], W["w_in"][l, k * 128:(k + 1) * 128, c0:c0 + 864],
                                  writes=[winAb])
                    cw, cwb = P.sb("cw", [128, 12, 3])
                    cb, cbb = P.sb("cb", [128, 12])
                    for t_ in range(3):
                        P.dma("sp", cw[:, :, t_], W["conv_w"][l, t_].rearrange("(j p) -> p j", p=128), writes=[cwb], **slow)
                    P.dma("sp", cb[:], W["conv_b"][l].rearrange("(j p) -> p j", p=128), writes=[cbb], **slow)
                    dtb, dtbb = P.sb("dtb", [128, 32])
                    Arow, Arowb = P.sb("Arow", [128, 32])
                    Drow, Drowb = P.sb("Drow", [128, 16])
                    ngbc, ngbcb = P.sb("ngbc", [128, D])
                    P.dma("sp", dtb[:], W["dt_bias"][l].rearrange("a b -> (a b)").partition_broadcast(128), writes=[dtbb])
                    P.dma("sp", Arow[:], W["a_log"][l].rearrange("a b -> (a b)").partition_broadcast(128), writes=[Arowb])
                    P.dma("sp", Drow[:], W["d_skip"][l].partition_broadcast(128), writes=[Drowb])
                    P.dma("sp", ngbc[:], W["ssd_norm_g"][l].partition_broadcast(128), writes=[ngbcb])
                    P.actv(Arow[:], Arow[:], AF.Exp, [Arowb], [Arowb])
                    P.ts("dve", Arow[:], Arow[:], -1.0, ALU.mult, [Arowb], [Arowb])

                    psA, psAb = P.ps("psA", [128, 3, 130])
                    tpp, tppb = P.ps("tpp", [128, 8, 128], BF16)
                    small, smallb = P.ps("small", [128, 512])
                    csrow, csrowb = P.ps("csrow", [128, 512])
                    big, bigb = P.ps("big", [128, 1024])
                    yps, ypsb = P.ps("yps", [128, 1024])

                    xcT, xcTb = P.sb("xcT", [128, 12, 128], BF16)
                    tcv, tcvb = P.sb("tcv", [128, 128])
                    xs, xsb = P.sb("xs", [128, 1024], BF16)
                    Btm, Btmb = P.sb("Btm", [128, 256], BF16)
                    dtx, dtxb = P.sb("dtx", [128, 32])
                    dt, dtb_ = P.sb("dt", [128, 32])
                    av, avb = P.sb("av", [128, 32])
                    cscol, cscolb = P.sb("cscol", [128, 32])
                    negcs, negcsb = P.sb("negcs", [128, 32])
                    dd, ddb = P.sb("dd", [128, 32])
                    dte, dteb = P.sb("dte", [128, 32])
                    ein, einb = P.sb("ein", [128, 32])
                    dtot, dtotb = P.sb("dtot", [128, 32])
                    wA, wAb = P.sb("wA", [128, 32])
                    xd0, xd0b = P.sb("xd0", [128, 1024], BF16)
                    xd1, xd1b = P.sb("xd1", [128, 1024], BF16)
                    xd = [(xd0, xd0b), (xd1, xd1b)]
                    Sst, Sstb = P.sb("Sst", [128, 1024])
                    Rst, Rstb = P.sb("Rst", [128, 1024])
                    sbf, sbfb = P.sb("sbf", [128, 1024], BF16)
                    t0, t0b = P.sb("t0", [128, 1024])
                    csb, csbb = t0, t0b
                    dtotB, dtotBb = P.sb("dtotB", [128, NT, 16])
                    P.memset("dve", Sst[:], 0.0, [Sstb])
                    P.memset("dve", Rst[:], 0.0, [Rstb])

                    def bc16(ap16):
                        return ap16.unsqueeze(2).to_broadcast([128, 16, 64])

                    def v3(t):
                        return t.rearrange("p (h d) -> p h d", h=16)

                    def front(i, passB):
                        col = tcol(i)
                        nj = 12 if passB else 10
                        for j0 in range(0, nj, 3):
                            js = list(range(j0, min(j0 + 3, nj)))
                            for jj, j in enumerate(js):
                                for k in range(8):
                                    P.mm(psA[:, jj, :], winA[:, k, XO + j * 128:XO + (j + 1) * 128],
                                         uT[:, k, col - 1:col + 129], k == 0, k == 7, [winAb] + uTr, [psAb])
                            for jj, j in enumerate(js):
                                P.ts("dve", tcv[:], psA[:, jj, 0:128], cw[:, j, 0:1], ALU.mult, [psAb, cwb], [tcvb])
                                P.stt(tcv[:], psA[:, jj, 1:129], cw[:, j, 1:2], tcv[:], ALU.mult, ALU.add,
                                      [psAb, cwb, tcvb], [tcvb])
                                P.stt(tcv[:], psA[:, jj, 2:130], cw[:, j, 2:3], tcv[:], ALU.mult, ALU.add,
                                      [psAb, cwb, tcvb], [tcvb])
                                P.actv(xcT[:, j, :], tcv[:], AF.Silu, [tcvb, cbb], [xcTb], bias=cb[:, j:j + 1])
                        for j in range(8):
                            P.tr(tpp[:, j, :], xcT[:, j, :], identb[:], [xcTb, identbb], [tppb])
                        P.cp("act", xs[:], tpp[:].rearrange("p a b -> p (a b)"), [tppb], [xsb])
                        for j in range(2):
                            P.tr(tpp[:, j, :], xcT[:, 8 + j, :], identb[:], [xcTb, identbb], [tppb])
                        P.cp("act", Btm[:], tpp[:, 0:2, :].rearrange("p a b -> p (a b)"), [tppb], [Btmb])
                        for k in range(8):
                            P.mm(small[:, 0:32], uT[:, k, col:col + 128], winA[:, k, DTO:DTO + 32], k == 0, k == 7,
                                 [winAb] + uTr, [smallb])
                        P.tt("dve", dtx[:], small[:, 0:32], dtb[:], ALU.add, [smallb, dtbb], [dtxb])
                        P.actv(dtx[:], dtx[:], AF.Exp, [dtxb], [dtxb])
                        P.actv(dt[:], dtx[:], AF.Ln, [dtxb], [dtb_], bias=1.0)
                        P.tt("dve", av[:], dt[:], Arow[:], ALU.mult, [dtb_, Arowb], [avb])
                        P.mm(small[:, 32:48], C("triF"), av[:, 0:16], True, True, [cstb, avb], [smallb])
                        P.mm(small[:, 48:64], C("triB"), av[:, 16:32], True, True, [cstb, avb], [smallb])
                        P.mm(small[:, 64:96], ones, av[:, 0:32], True, True, [cstb, avb], [smallb])
                        P.cp("dve", cscol[:], small[:, 32:64], [smallb], [cscolb])
                        P.ts("dve", negcs[:], cscol[:], -1.0, ALU.mult, [cscolb], [negcsb])
                        P.tt("dve", dd[:], small[:, 64:96], cscol[:], ALU.subtract, [smallb, cscolb], [ddb])
                        P.actv(dte[:], dd[:], AF.Exp, [ddb], [dteb])
                        P.actv(ein[:], cscol[:], AF.Exp, [cscolb], [einb])
                        P.actv(dtot[:], small[:, 64:96], AF.Exp, [smallb], [dtotb])

                    def passA(i):
                        front(i, False)
                        P.tt("dve", wA[:], dt[:], dte[:], ALU.mult, [dtb_, dteb], [wAb])
                        for d in range(2):
                            xdt, xdb = xd[d]
                            P.tt("dve", v3(xdt[:]), v3(xs[:]), bc16(wA[:, d * 16:(d + 1) * 16]), ALU.mult,
                                 [xsb, wAb], [xdb])
                        P.cp("act", sbf[:], Sst[:], [Sstb], [sbfb])
                        P.dma("pool", stS[0, i], sbf[:], reads=[sbfb], writes=[stB[0][i]])
                        for g in range(2):
                            P.mm(big[:, g * 512:(g + 1) * 512], Btm[:, g * 128:(g + 1) * 128],
                                 xd0[:, g * 512:(g + 1) * 512], True, True, [Btmb, xd0b], [bigb])
                        P.tt("dve", v3(Sst[:]), v3(Sst[:]), bc16(dtot[:, 0:16]), ALU.mult, [Sstb, dtotb], [Sstb])
                        P.tt("dve", Sst[:], big[:], Sst[:], ALU.add, [bigb, Sstb], [Sstb])
                        for g in range(2):
                            P.mm(big[:, g * 512:(g + 1) * 512], Btm[:, g * 128:(g + 1) * 128],
                                 xd1[:, g * 512:(g + 1) * 512], True, True, [Btmb, xd1b], [bigb])
                        P.cp("act", csb[:], big[:], [bigb], [csbb])
                        P.dma("pool", csbS[i], csb[:], reads=[csbb], writes=[csbB[i]])
                        P.cp("dve", dtotB[:, i, :], dtot[:, 16:32], [dtotb], [dtotBb])

                    def rsweep(tl):
                        for i in reversed(tl):
                            P.cp("act", sbf[:], Rst[:], [Rstb], [sbfb])
                            P.dma("pool", stS[1, i], sbf[:], reads=[sbfb], writes=[stB[1][i]])
                            P.dma("sp", csb[:], csbS[i], reads=[csbB[i]], writes=[csbb])
                            P.tt("dve", v3(Rst[:]), v3(Rst[:]), bc16(dtotB[:, i, :]), ALU.mult, [Rstb, dtotBb], [Rstb])
                            P.tt("dve", Rst[:], csb[:], Rst[:], ALU.add, [csbb, Rstb], [Rstb])

                    cbT, cbTb = P.sb("cbT", [128, 256])
                    Ua, Uab = P.sb("Ua", [128, 4, 128])
                    LT, LTb = P.sb("LT", [128, 128])
                    MT, MTb = P.sb("MT", [128, 32, 128], BF16)
                    Sl0, Sl0b = P.sb("Sl0", [128, 1024], BF16)
                    Sl1, Sl1b = P.sb("Sl1", [128, 1024], BF16)
                    Sl = [(Sl0, Sl0b), (Sl1, Sl1b)]
                    t1, t1b = P.sb("t1", [128, 1024])
                    ssq, ssqb = P.sb("ssq", [128, 1])
                    yn, ynb = P.sb("yn", [128, 1024], BF16)

                    def passB(i):
                        front(i, True)
                        col = tcol(i)
                        for d in range(2):
                            P.dma("sp", Sl[d][0][:], stS[d, i], reads=[stB[d][i]], writes=[Sl[d][1]])
                        for g in range(2):
                            P.mm(small[:, 128 + g * 128:256 + g * 128], xcT[:, 8 + g, :], xcT[:, 10 + g, :], True, True,
                                 [xcTb], [smallb])
                        P.cp("dve", cbT[:], small[:, 128:384], [smallb], [cbTb])
                        for d in range(2):
                            xdt, xdb = xd[d]
                            P.tt("dve", v3(xdt[:]), v3(xs[:]), bc16(dt[:, d * 16:(d + 1) * 16]), ALU.mult,
                                 [xsb, dtb_], [xdb])
                        for d in range(2):
                            tri = C("triF") if d == 0 else C("triB")
                            nm = C("nmF") if d == 0 else C("nmB")
                            for hb in range(4):
                                P.tt("dve", Ua[:], av[:, d * 16 + hb * 4:d * 16 + hb * 4 + 4].unsqueeze(2).to_broadcast([128, 4, 128]),
                                     tri.unsqueeze(1).to_broadcast([128, 4, 128]), ALU.mult, [avb, cstb], [Uab])
                                P.mm(csrow[:], ones, Ua[:].rearrange("p a b -> p (a b)"), True, False, [cstb, Uab], [csrowb])
                                P.mm(csrow[:], identf, nm, False, True, [cstb], [csrowb])
                                for hh in range(4):
                                    h = hb * 4 + hh
                                    P.actv(LT[:], csrow[:, hh * 128:(hh + 1) * 128], AF.Exp, [csrowb, negcsb], [LTb],
                                           bias=negcs[:, d * 16 + h:d * 16 + h + 1])
                                    g = h // 8
                                    P.tt("dve", MT[:, d * 16 + h, :], LT[:], cbT[:, g * 128:(g + 1) * 128], ALU.mult,
                                         [LTb, cbTb], [MTb])
                        for h in range(16):
                            P.mm(yps[:, h * 64:(h + 1) * 64], MT[:, h, :], xd0[:, h * 64:(h + 1) * 64], True, False,
                                 [MTb, xd0b], [ypsb])
                            P.mm(yps[:, h * 64:(h + 1) * 64], MT[:, 16 + h, :], xd1[:, h * 64:(h + 1) * 64], False, True,
                                 [MTb, xd1b], [ypsb])
                        for d in range(2):
                            Sd, Sdb = Sl[d]
                            for g in range(2):
                                P.mm(big[:, g * 512:(g + 1) * 512], xcT[:, 10 + g, :], Sd[:, g * 512:(g + 1) * 512],
                                     True, True, [xcTb, Sdb], [bigb])
                            tdst, tdstb = (t0, t0b) if d == 0 else (t1, t1b)
                            P.tt("dve", v3(tdst[:]), v3(big[:]), bc16(ein[:, d * 16:(d + 1) * 16]), ALU.mult,
                                 [bigb, einb], [tdstb])
                        P.tt("dve", t0[:], t0[:], t1[:], ALU.add, [t0b, t1b], [t0b])
                        P.tt("dve", t0[:], yps[:], t0[:], ALU.add, [ypsb, t0b], [t0b])
                        P.tt("dve", v3(t1[:]), v3(xs[:]), bc16(Drow[:]), ALU.mult, [xsb, Drowb], [t1b])
                        P.tt("dve", t0[:], t0[:], t1[:], ALU.add, [t0b, t1b], [t0b])
                        for n in range(2):
                            for k in range(8):
                                P.mm(big[:, n * 512:(n + 1) * 512], uT[:, k, col:col + 128],
                                     winA[:, k, ZO + n * 512:ZO + (n + 1) * 512], k == 0, k == 7, [winAb] + uTr, [bigb])
                        P.actv(t1[:], big[:], AF.Silu, [bigb], [t1b])
                        P.tt("dve", t0[:], t0[:], t1[:], ALU.mult, [t0b, t1b], [t0b])
                        P.actv(t1[:], t0[:], AF.Square, [t0b], [t1b, ssqb], accum=ssq[:])
                        P.actv(ssq[:], ssq[:], AF.Sqrt, [ssqb], [ssqb], bias=1e-6, scale=1.0 / D)
                        P.op("dve", lambda e: e.reciprocal(out=ssq[:], in_=ssq[:]), [ssqb], [ssqb])
                        P.stt(yn[:], t0[:], ssq[:, 0:1], ngbc[:], ALU.mult, ALU.mult, [t0b, ssqb, ngbcb], [ynb])
                        P.dma("pool", mixS[i * 128:(i + 1) * 128, 0:1024], yn[:], reads=[ynb], writes=[mixB[i]])

                    for i in (0, 1):
                        passA(i)
                    rsweep([0, 1])
                    if not last:
                        for i in (0, 1):
                            passB(i)
                    for i in range(2, NT):
                        passA(i)
                    rsweep(list(range(2, NT)))
                    for i in range(2, NT):
                        passB(i)

                with P.scope():
                    winB, winBb = P.sb("winB", [128, 8, 2048], BF16)
                    for k in range(8):
                        for c0 in range(0, 2048, 1024):
                            P.dma("pool", winB[:, k, c0:c0 + 1024],
                                  W["w_in"][l, k * 128:(k + 1) * 128, PLO + c0:PLO + c0 + 1024], writes=[winBb])
                    wpl, wplb = P.sb("wpl", [128, 4, 128], BF16)
                    P.dma("pool", wpl[:], W["pool_w"][l].rearrange("g i o -> i g o"), writes=[wplb])
                    psc, pscb = P.sb("psc", [128, 512])
                    P.dma("sp", psc[:], W["pool_scale"][l].partition_broadcast(128), writes=[pscb])
                    ptab, ptabb = P.sb("ptab", [128, NCST - NCOM])
                    P.dma("sp", ptab[:], cst_in[:, NCOM:NCST], writes=[ptabb])
                    oMA = SEC["MA"][0] - NCOM
                    oMB = SEC["MB"][0] - NCOM
                    MA = ptab[:, oMA:oMA + 1536].rearrange("p (v g t) -> p v g t", v=3, g=4)
                    MB = ptab[0:16, oMB:oMB + 1536].rearrange("p (v g t) -> p v g t", v=3, g=4)

                    with P.scope():
                        plAp, plApb = P.ps("plAp", [128, 512])
                        plBp, plBpb = P.ps("plBp", [16, 512])
                        ppp, pppb = P.ps("ppp", [128, 512])
                        ypp, yppb = P.ps("ypp", [128, 512])
                        plA, plAb = P.sb("plA", [128, 512])
                        plB, plBb = P.sb("plB", [16, 512])
                        pT, pTb = P.sb("pT", [128, 512], BF16)
                        yo, yob = P.sb("yo", [128, 512], BF16)
                        for i in out_tiles:
                            col = tcol(i)
                            v = 1 if i in (0, 2) else (2 if i in (1, NT - 1) else 0)
                            for k in range(8):
                                P.mm(plAp[:], uT[:, k, col - 8:col + 120], winB[:, k, 0:512], k == 0, k == 7,
                                     [winBb] + uTr, [plApb])
                            for k in range(8):
                                P.mm(plBp[:], uT[:, k, col + 120:col + 136], winB[:, k, 0:512], k == 0, k == 7,
                                     [winBb] + uTr, [plBpb])
                            P.cp("act", plA[:], plAp[:], [plApb], [plAb])
                            P.cp("dve", plB[:], plBp[:], [plBpb], [plBb])
                            for g in range(4):
                                P.mm(ppp[:, g * 128:(g + 1) * 128], plA[:, g * 128:(g + 1) * 128], MA[:, v, g, :], True, False,
                                     [plAb, ptabb], [pppb])
                                P.mm(ppp[:, g * 128:(g + 1) * 128], plB[:, g * 128:(g + 1) * 128], MB[:, v, g, :], False, True,
                                     [plBb, ptabb], [pppb])
                            P.cp("act", pT[:], ppp[:], [pppb], [pTb])
                            for g in range(4):
                                P.mm(ypp[:, g * 128:(g + 1) * 128], pT[:, g * 128:(g + 1) * 128], wpl[:, g, :], True, True,
                                     [pTb, wplb], [yppb])
                            P.tt("dve", yo[:], ypp[:], psc[:], ALU.mult, [yppb, pscb], [yob])
                            P.dma("pool", mixS[i * 128:(i + 1) * 128, 1024:1536], yo[:], reads=[yob], writes=[mixB[i]])

                    with P.scope():
                        QW, KW, VW = QO - PLO, KO - PLO, VO - PLO
                        BT, BTb = P.sb("BT", [64, 120, 64], BF16)
                        ident64 = identb[0:64, 0:64]
                        with P.scope():
                            Hf, Hfb = P.sb("Hf", [64, 120, 64])
                            P.dma("sp", rpbP[:, :], zero[0:120, :], reads=[zerob], writes=[rpbB])
                            rev = bass.AP(W["na_rpb"].tensor, l * 120 * 31 + 30, [[31, 120], [-1, 31]])
                            P.dma("sp", rpbP[:, 48:79], rev, writes=[rpbB], allow_slow_non_contiguous=True)
                            hank = bass.AP(rpbP.tensor, 0, [[1, 64], [160, 120], [1, 64]])
                            P.dma("sp", Hf[:], hank, reads=[rpbB], writes=[Hfb])
                            btp, btpb = P.ps("btp", [64, 8, 64])
                            for bi in range(15):
                                P.mm(btp[:], C("J64", 64), Hf[:, bi * 8:(bi + 1) * 8, :], True, True, [cstb, Hfb], [btpb])
                                P.tt("dve", BT[:, bi * 8:(bi + 1) * 8, :], btp[:],
                                     C("cmask", 64).unsqueeze(1).to_broadcast([64, 8, 64]), ALU.add, [btpb, cstb], [BTb])
                        kps, kpsb = P.ps("kps", [64, 8, 128])
                        vps, vpsb = P.ps("vps", [64, 2, 512])
                        sc, scb = P.ps("sc", [64, 1024])
                        o0, o0b = P.ps("o0", [64, 4, 65])
                        o1, o1b = P.ps("o1", [64, 4, 65])
                        KTc, KTcb = P.sb("KTc", [64, 8, 256], BF16)
                        Vc, Vcb = P.sb("Vc", [64, 4, 8, 65], BF16)
                        kt, ktb_ = P.sb("kt", [64, 8, 128], BF16)
                        vt, vtb = P.sb("vt", [64, 2, 8, 65], BF16)
                        QT, QTb = P.sb("QT", [64, 8, 128], BF16)
                        KTw, KTwb = P.sb("KTw", [64, 8, 576], BF16)
                        Vw, Vwb = P.sb("Vw", [64, 9, 520], BF16)
                        PT, PTb = P.sb("PT", [64, 768], BF16)
                        rinv, rinvb = P.sb("rinv", [64, 8])
                        nao, naob = P.sb("nao", [64, 512], BF16)
                        P.memset("dve", vt[:], 1.0, [vtb])
                        P.memset("dve", Vc[:], 1.0, [Vcb])

                        def proj_T(dst, dstb, off, i, scale=None):
                            col = tcol(i)
                            for h in range(8):
                                for k in range(8):
                                    P.mm(kps[:, h, :], winB[:, k, off + h * 64:off + (h + 1) * 64], uT[:, k, col:col + 128],
                                         k == 0, k == 7, [winBb] + uTr, [kpsb])
                            if scale is None:
                                P.cp("act", dst, kps[:], [kpsb], [dstb])
                            else:
                                P.actv(dst, kps[:], AF.Identity, [kpsb], [dstb], scale=scale)

                        def proj_V(dst4, dstb, i):
                            col = tcol(i)
                            for rr in range(2):
                                for k in range(8):
                                    P.mm(vps[:, rr, :], uT[:, k, col + rr * 64:col + (rr + 1) * 64], winB[:, k, VW:VW + 512],
                                         k == 0, k == 7, [winBb] + uTr, [vpsb])
                            P.cp("dve", dst4, vps[:].rearrange("p r (h d) -> p r h d", h=8), [vpsb], [dstb])

                        for i in (0, 1):
                            proj_T(KTc[:, :, i * 128:(i + 1) * 128], KTcb, KW, i)
                            proj_V(Vc[:, 2 * i:2 * i + 2, :, 0:64], Vcb, i)
                        for i in range(2, NT):
                            j = i - 2
                            proj_T(kt[:], ktb_, KW, i)
                            P.dma("pool", KTs[:, :, j * 128:(j + 1) * 128], kt[:], reads=[ktb_], writes=[ktB[j]])
                            proj_V(vt[:, :, :, 0:64], vtb, i)
                            P.dma("pool", Vs[2 * j:2 * j + 2].rearrange("r k e -> k r e"),
                                  vt[:].rearrange("p r h e -> p r (h e)"), reads=[vtb], writes=[vB[j]])

                        def attend(i):
                            isctx = i < 2
                            proj_T(QT[:], QTb, QW, i, scale=0.125)
                            if not isctx:
                                j = i - 2
                                r0s = [min(max(2 * j + rr - 4, 0), 56) for rr in range(2)]
                                rlo, rhi = r0s[0], r0s[1] + 8
                                nr = rhi - rlo
                                tl = sorted(set(r // 2 for r in range(rlo, rhi)))
                                P.dma("sp", KTw[:, :, 0:nr * 64], KTs[:, :, rlo * 64:rhi * 64], reads=[ktB[t] for t in tl],
                                      writes=[KTwb])
                                P.dma("sp", Vw[:, 0:nr, :], Vs[rlo:rhi].rearrange("r k e -> k r e"), reads=[vB[t] for t in tl],
                                      writes=[Vwb])
                            for rr in range(2):
                                qs = QT
                                for h in range(8):
                                    q_ap = QT[:, h, rr * 64:(rr + 1) * 64]
                                    nloc = 0
                                    if not isctx:
                                        r = 2 * (i - 2) + rr
                                        r0 = r0s[rr]
                                        dr0 = r0 - r + 7
                                        nloc = 8
                                        P.mm(sc[:, 0:512], ident64, BT[:, h * 15 + dr0:h * 15 + dr0 + 8, :], True, False,
                                             [identbb, BTb], [scb])
                                        for jj in range(8):
                                            off = (r0 + jj - rlo) * 64
                                            P.mm(sc[:, jj * 64:(jj + 1) * 64], KTw[:, h, off:off + 64], q_ap, False, jj == 7,
                                                 [KTwb, QTb], [scb])
                                    for jc in range(4):
                                        P.mm(sc[:, 512 + jc * 64:512 + (jc + 1) * 64], KTc[:, h, jc * 64:(jc + 1) * 64], q_ap,
                                             True, True, [KTcb, QTb], [scb])
                                    lo = 0 if not isctx else 512
                                    P.actv(PT[:, lo:768], sc[:, lo:768], AF.Exp, [scb], [PTb])
                                    ot, otb = (o0, o0b) if h < 4 else (o1, o1b)
                                    blocks = []
                                    for jj in range(nloc):
                                        blocks.append((PT[:, jj * 64:(jj + 1) * 64],
                                                       Vw[:, r0 + jj - rlo, h * 65:(h + 1) * 65], Vwb))
                                    for jc in range(4):
                                        blocks.append((PT[:, 512 + jc * 64:512 + (jc + 1) * 64], Vc[:, jc, h, :], Vcb))
                                    for bi, (pa, va, vb_) in enumerate(blocks):
                                        P.mm(ot[:, h % 4, :], pa, va, bi == 0, bi == len(blocks) - 1, [PTb, vb_], [otb])
                                for half, (ot, otb) in enumerate(((o0, o0b), (o1, o1b))):
                                    P.op("dve", lambda e, ot=ot, half=half: e.reciprocal(
                                        out=rinv[:, half * 4:(half + 1) * 4], in_=ot[:, :, 64]), [otb], [rinvb])
                                    P.tt("dve", nao[:, half * 256:(half + 1) * 256].rearrange("p (h d) -> p h d", h=4),
                                         ot[:, :, 0:64],
                                         rinv[:, half * 4:(half + 1) * 4].unsqueeze(2).to_broadcast([64, 4, 64]),
                                         ALU.mult, [otb, rinvb], [naob])
                                row = i * 128 + rr * 64
                                P.dma("pool", mixS[row:row + 64, 1536:2048], nao[:], reads=[naob], writes=[mixB[i]])

                        for i in out_tiles:
                            attend(i)

                mix_scope.__exit__(None, None, None)
                gbc, gbcb = P.sb("gbc", [128, 2, 2, 1024])
                P.dma("sp", gbc[:].rearrange("p a b c -> p (a b c)"), gS[:, :], reads=[gSb], writes=[gbcb])
                with P.scope():
                    wout, woutb = P.sb("wout", [128, 16, 1024], BF16)
                    for f in range(16):
                        P.dma("pool", wout[:, f, :], W["w_out"][l, f * 128:(f + 1) * 128, :], writes=[woutb])
                    tpA, tpAb = P.ps("tpA", [128, 8, 128], BF16)
                    outp, outpb = P.ps("outp", [128, 1024])
                    mt, mtb = P.sb("mt", [128, 2048], BF16)
                    mixT, mixTb = P.sb("mixT", [128, 16, 128], BF16)
                    h_t, h_tb = P.sb("h_t", [128, D])
                    hm, hmb = P.sb("hm", [128, D])
                    for i in out_tiles:
                        r = 1 if i < 2 else 0
                        P.dma("sp", mt[:], mixS[i * 128:(i + 1) * 128, :], reads=[mixB[i]], writes=[mtb])
                        if l == 0:
                            src = ctx_in[i * 128:(i + 1) * 128, :] if i < 2 else x_in[(i - 2) * 128:(i - 1) * 128, :]
                            rds = []
                        else:
                            src = hS[i * 128:(i + 1) * 128, :]
                            rds = [hB[i]]
                        P.dma("sp", h_t[:], src, reads=rds, writes=[h_tb])
                        if dbg and "mix" in dbg and l == 0:
                            for hh2 in range(2):
                                P.cp("dve", hm[:], mt[:, hh2 * 1024:(hh2 + 1) * 1024], [mtb], [hmb])
                                P.dma("sp", dbg_out["mix"][i * 128:(i + 1) * 128, hh2 * 1024:(hh2 + 1) * 1024], hm[:], reads=[hmb], writes=[dbgB])
                        for hf in range(2):
                            for f in range(8):
                                P.tr(tpA[:, f, :], mt[:, (hf * 8 + f) * 128:(hf * 8 + f + 1) * 128], identb[:], [mtb, identbb], [tpAb])
                            P.cp("act", mixT[:, hf * 8:(hf + 1) * 8, :], tpA[:], [tpAb], [mixTb])
                        for n in range(2):
                            for f in range(16):
                                P.mm(outp[:, n * 512:(n + 1) * 512], mixT[:, f, :], wout[:, f, n * 512:(n + 1) * 512],
                                     f == 0, f == 15, [mixTb, woutb], [outpb])
                        P.tt("dve", hm[:], outp[:], gbc[:, 0, r, :], ALU.mult, [outpb, gbcb], [hmb])
                        P.tt("dve", hm[:], hm[:], h_t[:], ALU.add, [hmb, h_tb], [hmb])
                        P.dma("sp", hS[i * 128:(i + 1) * 128, :], hm[:], reads=[hmb], writes=[hB[i]])
                        if dbg and "hmid" in dbg and l == 0:
                            P.dma("sp", dbg_out["hmid"][i * 128:(i + 1) * 128, :], hm[:], reads=[hmb], writes=[dbgB])
                if dbg and dbg.get("_stop") == "mix":
                    break
                with P.scope():
                    wq, wqb = P.sb("wq", [128, 8, 2048])
                    P.dma("sp", wq[:], W["peer_wq"][l].rearrange("(k p) n -> p k n", p=128), writes=[wqb])
                    keysT, keysTb = P.sb("keysT", [128, 16, 128])
                    fgbc, fgbcb = P.sb("fgbc", [128, D])
                    if last:
                        P.dma("sp", fgbc[:], W["final_g"].partition_broadcast(128), writes=[fgbcb])
                    outp, outpb = P.ps("outp", [128, 1024])
                    tpF, tpFb = P.ps("tpF", [128, 8, 128])
                    qps, qpsb = P.ps("qps", [128, 4, 128])
                    sps, spsb = P.ps("sps", [128, 4, 128])
                    with P.scope():
                        kraw, krawb = P.sb("kraw", [128, 16, 128])
                        P.dma("sp", kraw[:], W["peer_keys"][l].rearrange("h i k d -> k (h i) d"), writes=[krawb])
                        for j in range(16):
                            P.tr(tpF[:, j % 8, :], kraw[:, j, :], identf, [krawb, cstb], [tpFb])
                            if j % 8 == 7:
                                P.cp("dve", keysT[:, j - 7:j + 1, :], tpF[:], [tpFb], [keysTb])
                    hm, hmb = P.sb("hm", [128, D])
                    jk, jkb = P.sb("jk", [128, D])
                    s1, s1b = P.sb("s1", [128, 1])
                    xn2, xn2b = P.sb("xn2", [128, D])
                    u2T, u2Tb = P.sb("u2T", [128, 8, 128])
                    u2, u2b = P.sb("u2", [128, D])
                    big1, big1b = P.sb("big1", [128, 2048])
                    big2, big2b = P.sb("big2", [128, 2048])
                    qT = big1[:].rearrange("p (j t) -> p j t", j=16)
                    cidx = big1[:].rearrange("p (h c) -> p h c", h=8)
                    s_all = big2[:].rearrange("p (j t) -> p j t", j=16)
                    cand = big2[:].rearrange("p (h c) -> p h c", h=8)
                    qTb = cidxb = big1b
                    s_allb = candb = big2b
                    stmp, stmpb = P.sb("stmp", [128, 128])
                    v12, v12b = P.sb("v12", [128, 16, 16])
                    i12, i12b = P.sb("i12", [128, 16, 16], U32)
                    i12f, i12fb = P.sb("i12f", [128, 16, 16])
                    ctmp, ctmpb = P.sb("ctmp", [128, 256])
                    tops, topsb = P.sb("tops", [128, 8, 16])
                    pos, posb = P.sb("pos", [128, 8, 16], U32)
                    posf, posfb = P.sb("posf", [128, 8, 16])
                    eidf, eidfb = P.sb("eidf", [128, 128])
                    eid, eidb = P.sb("eid", [128, 128], I32)
                    nmax, nmaxb = P.sb("nmax", [128, 8])
                    eg, egb = P.sb("eg", [128, 8, 16])
                    esum, esumb = P.sb("esum", [128, 8])
                    gate, gateb = P.sb("gate", [128, 128])
                    actt, acttb = P.sb("actt", [128, 128])
                    wgt, wgtb = P.sb("wgt", [128, 128])
                    NB = 4
                    ub = [P.sb(f"ub{n}", [128, D]) for n in range(NB)]
                    dg = [P.sb(f"dg{n}", [128, 128]) for n in range(2)]
                    ho, hob = P.sb("ho", [128, D])
                    iota = C("iota")

                    for i in out_tiles:
                        r = 1 if i < 2 else 0
                        P.dma("sp", hm[:], hS[i * 128:(i + 1) * 128, :], reads=[hB[i]], writes=[hmb])
                        P.actv(jk[:], hm[:], AF.Square, [hmb], [jkb, s1b], accum=s1[:])
                        P.actv(s1[:], s1[:], AF.Sqrt, [s1b], [s1b], bias=1e-6, scale=1.0 / D)
                        P.op("dve", lambda e: e.reciprocal(out=s1[:], in_=s1[:]), [s1b], [s1b])
                        P.ts("dve", xn2[:], hm[:], s1[:, 0:1], ALU.mult, [hmb, s1b], [xn2b])
                        for k in range(8):
                            P.tr(tpF[:, k, :], xn2[:, k * 128:(k + 1) * 128], identf, [xn2b, cstb], [tpFb])
                        for k in range(8):
                            P.actv(u2T[:, k, :], tpF[:, k, :], AF.Identity, [tpFb, A2b, modTb], [u2Tb],
                                   bias=B2[:, k, r:r + 1], scale=A2[:, k, r:r + 1])
                        for k in range(8):
                            P.tr(tpF[:, k, :], u2T[:, k, :], identf, [u2Tb, cstb], [tpFb])
                        P.cp("act", u2[:], tpF[:].rearrange("p a b -> p (a b)"), [tpFb], [u2b])
                        for j4 in range(4):
                            for jj in range(4):
                                j = j4 * 4 + jj
                                for k in range(8):
                                    P.mm(qps[:, jj, :], wq[:, k, j * 128:(j + 1) * 128], u2T[:, k, :], k == 0, k == 7,
                                         [wqb, u2Tb], [qpsb])
                            P.cp("dve", qT[:, j4 * 4:(j4 + 1) * 4, :], qps[:], [qpsb], [qTb])
                        for j4 in range(4):
                            for jj in range(4):
                                j = j4 * 4 + jj
                                P.mm(sps[:, jj, :], qT[:, j, :], keysT[:, j, :], True, True, [qTb, keysTb], [spsb])
                            P.cp("act", s_all[:, j4 * 4:(j4 + 1) * 4, :], sps[:], [spsb], [s_allb])
                        for j in range(16):
                            P.op("dve", lambda e, j=j: e.max(out=v12[:, j, 0:8], in_=s_all[:, j, :]), [s_allb], [v12b])
                            P.op("dve", lambda e, j=j: e.max_index(out=i12[:, j, 0:8], in_max=v12[:, j, 0:8], in_values=s_all[:, j, :]),
                                 [s_allb, v12b], [i12b])
                            P.op("dve", lambda e, j=j: e.match_replace(out=stmp[:], in_to_replace=v12[:, j, 0:8],
                                                                      in_values=s_all[:, j, :], imm_value=-1e30),
                                 [s_allb, v12b], [stmpb])
                            P.op("dve", lambda e, j=j: e.max(out=v12[:, j, 8:16], in_=stmp[:]), [stmpb], [v12b])
                            P.op("dve", lambda e, j=j: e.max_index(out=i12[:, j, 8:16], in_max=v12[:, j, 8:16], in_values=s_all[:, j, :]),
                                 [s_allb, v12b], [i12b])
                        P.cp("dve", i12f[:], i12[:], [i12b], [i12fb])
                        vv = v12[:].rearrange("p (h t) k -> p h t k", t=2)
                        ii = i12f[:].rearrange("p (h t) k -> p h t k", t=2)
                        P.ts("dve", ii[:, :, 0, :], ii[:, :, 0, :], 128.0, ALU.mult, [i12fb], [i12fb])
                        c4 = cand.rearrange("p h (a b) -> p h a b", a=16)
                        x4 = cidx.rearrange("p h (a b) -> p h a b", a=16)
                        P.tt("dve", c4, vv[:, :, 0, :].unsqueeze(3).to_broadcast([128, 8, 16, 16]),
                             vv[:, :, 1, :].unsqueeze(2).to_broadcast([128, 8, 16, 16]), ALU.add, [v12b], [candb])
                        P.tt("dve", x4, ii[:, :, 0, :].unsqueeze(3).to_broadcast([128, 8, 16, 16]),
                             ii[:, :, 1, :].unsqueeze(2).to_broadcast([128, 8, 16, 16]), ALU.add, [i12fb], [cidxb])
                        for h in range(8):
                            P.op("dve", lambda e, h=h: e.max(out=tops[:, h, 0:8], in_=cand[:, h, :]), [candb], [topsb])
                            P.op("dve", lambda e, h=h: e.max_index(out=pos[:, h, 0:8], in_max=tops[:, h, 0:8], in_values=cand[:, h, :]),
                                 [candb, topsb], [posb])
                            P.op("dve", lambda e, h=h: e.match_replace(out=ctmp[:], in_to_replace=tops[:, h, 0:8],
                                                                      in_values=cand[:, h, :], imm_value=-1e30),
                                 [candb, topsb], [ctmpb])
                            P.op("dve", lambda e, h=h: e.max(out=tops[:, h, 8:16], in_=ctmp[:]), [ctmpb], [topsb])
                            P.op("dve", lambda e, h=h: e.max_index(out=pos[:, h, 8:16], in_max=tops[:, h, 8:16], in_values=cand[:, h, :]),
                                 [candb, topsb], [posb])
                        P.cp("dve", posf[:], pos[:], [posb], [posfb])
                        for h in range(8):
                            for k in range(16):
                                P.stt(ctmp[:], iota, posf[:, h, k:k + 1], cidx[:, h, :], ALU.is_equal, ALU.mult,
                                      [cstb, posfb, cidxb], [ctmpb, eidfb], accum=eidf[:, h * 16 + k:h * 16 + k + 1])
                        P.ts("dve", eidf[:], eidf[:], 16383.0, ALU.min, [eidfb], [eidfb], s2=0.0, op1=ALU.max)
                        if l > 0:
                            P.ts("dve", eidf[:], eidf[:], float(l * 16384), ALU.add, [eidfb], [eidfb])
                        P.cp("dve", eid[:], eidf[:], [eidfb], [eidb])
                        P.ts("dve", nmax[:], tops[:, :, 0], -1.0, ALU.mult, [topsb], [nmaxb])
                        for h in range(8):
                            P.actv(eg[:, h, :], tops[:, h, :], AF.Exp, [topsb, nmaxb], [egb, esumb], bias=nmax[:, h:h + 1],
                                   accum=esum[:, h:h + 1])
                        P.op("dve", lambda e: e.reciprocal(out=esum[:], in_=esum[:]), [esumb], [esumb])
                        P.tt("dve", gate[:].rearrange("p (h k) -> p h k", h=8), eg[:],
                             esum[:].unsqueeze(2).to_broadcast([128, 8, 16]), ALU.mult, [egb, esumb], [gateb])
                        for hk in range(128):
                            ut, utb = ub[hk % NB]
                            P.idma(ut[:], W["peer_u"].rearrange("l e d -> (l e) d"), eid[:, hk:hk + 1], reads=[eidb], writes=[utb])
                            P.stt(jk[:], ut[:], 1.0, u2[:], ALU.mult, ALU.mult, [utb, u2b], [jkb, acttb],
                                  accum=actt[:, hk:hk + 1])
                        P.actv(wgt[:], actt[:], AF.Gelu, [acttb], [wgtb])
                        P.tt("dve", wgt[:], wgt[:], gate[:], ALU.mult, [wgtb, gateb], [wgtb])
                        for hk in range(128):
                            ut, utb = ub[hk % NB]
                            dgt, dgb = dg[hk % 2]
                            P.idma(ut[:], W["peer_v"].rearrange("l e d -> (l e) d"), eid[:, hk:hk + 1], reads=[eidb], writes=[utb])
                            P.ts("dve", dgt[:], identf, wgt[:, hk:hk + 1], ALU.mult, [cstb, wgtb], [dgb])
                            for n in range(2):
                                P.mm(outp[:, n * 512:(n + 1) * 512], dgt[:], ut[:, n * 512:(n + 1) * 512], hk == 0, hk == 127,
                                     [dgb, utb], [outpb])
                        if dbg and "peer" in dbg and l == 0:
                            P.cp("act", ho[:], outp[:], [outpb], [hob])
                            P.dma("sp", dbg_out["peer"][i * 128:(i + 1) * 128, :], ho[:], reads=[hob], writes=[dbgB])
                        P.tt("dve", ho[:], outp[:], gbc[:, 1, r, :], ALU.mult, [outpb, gbcb], [hob])
                        P.tt("dve", ho[:], ho[:], hm[:], ALU.add, [hob, hmb], [hob])
                        if dbg and "h0" in dbg and l == 0:
                            P.dma("sp", dbg_out["h0"][i * 128:(i + 1) * 128, :], ho[:], reads=[hob], writes=[dbgB])
                        if not last:
                            P.dma("sp", hS[i * 128:(i + 1) * 128, :], ho[:], reads=[hob], writes=[hB[i]])
                        else:
                            P.actv(jk[:], ho[:], AF.Square, [hob], [jkb, s1b], accum=s1[:])
                            P.actv(s1[:], s1[:], AF.Sqrt, [s1b], [s1b], bias=1e-6, scale=1.0 / D)
                            P.op("dve", lambda e: e.reciprocal(out=s1[:], in_=s1[:]), [s1b], [s1b])
                            P.stt(xn2[:], ho[:], s1[:, 0:1], fgbc[:], ALU.mult, ALU.mult, [hob, s1b, fgbcb], [xn2b])
                            P.dma("sp", y_out[(i - 2) * 128:(i - 1) * 128, :], xn2[:], reads=[xn2b], writes=[yB])
        P.barrier()
        print("program: instr", P.ninstr, "waits", P.nwaits, "per-engine", P.cnt)
    return nc


_NC_CACHE = {}


def kernel(**inputs):
    n = 4
    if "nc" not in _NC_CACHE:
        _NC_CACHE["nc"] = build_program()
    nc = _NC_CACHE["nc"]
    shared = {name: np.ascontiguousarray(inputs[name], dtype=np.float32) for name, _ in WSPEC}
    shared["cst"] = CST
    in_maps = []
    for b in range(n):
        m = dict(shared)
        m["x"] = np.ascontiguousarray(inputs["x"][b], dtype=np.float32)
        m["ctx"] = np.ascontiguousarray(inputs["ctx"][b], dtype=np.float32)
        m["c2"] = np.ascontiguousarray(np.stack([inputs["c"][b], inputs["c_ctx"]]), dtype=np.float32)
        in_maps.append(m)
    res = run_bass_kernel_spmd(nc, in_maps, core_ids=list(range(n)))
    return np.stack([res.results[b]["y"] for b in range(n)], axis=0).astype(np.float32)
```

```python
import numpy as np
from contextlib import ExitStack, contextmanager
import concourse.bass as bass
import concourse.mybir as mybir
from concourse.bass_utils import run_bass_kernel_spmd

F32 = mybir.dt.float32
BF16 = mybir.dt.bfloat16
U32 = mybir.dt.uint32
I32 = mybir.dt.int32
AF = mybir.ActivationFunctionType
ALU = mybir.AluOpType

EPOCH = 30000
NSEM_PER_ENG = 14
NDMA_SEM = 40


class Buf:
    __slots__ = ("name", "lw", "rd")

    def __init__(self, name=""):
        self.name = name
        self.lw = None
        self.rd = {}


class Prog:
    def __init__(self, nc, stack):
        self.nc = nc
        self.stacks = [stack]
        self.engs = {"pe": nc.tensor, "dve": nc.vector, "act": nc.scalar,
                     "pool": nc.gpsimd, "sp": nc.sync}
        self.cnt = {e: 0 for e in self.engs}
        self.sems = {e: [] for e in self.engs}
        self.dma_sems = [stack.enter_context(nc.semaphore(f"s_dma_{i}")) for i in range(NDMA_SEM)]
        self.dma_val = [0] * NDMA_SEM
        self.dma_next = 0
        self.known = {e: {f: 0 for f in self.engs} for e in self.engs}
        self.known_dma = {e: [0] * NDMA_SEM for e in self.engs}
        self.nwaits = 0
        self.ninstr = 0
        self.uid = 0

    def esem(self, e, ep):
        while len(self.sems[e]) <= ep:
            self.sems[e].append(self.stacks[0].enter_context(self.nc.semaphore(f"s_{e}_{len(self.sems[e])}")))
        return self.sems[e][ep]

    def sb(self, name, shape, dt=F32):
        self.uid += 1
        t = self.stacks[-1].enter_context(self.nc.sbuf_tensor(f"{name}_{self.uid}", list(shape), dt))
        return t, Buf(name)

    def ps(self, name, shape, dt=F32):
        self.uid += 1
        t = self.stacks[-1].enter_context(self.nc.psum_tensor(f"{name}_{self.uid}", list(shape), dt))
        return t, Buf(name)

    @contextmanager
    def scope(self):
        st = ExitStack()
        self.stacks.append(st)
        try:
            yield
        finally:
            self.barrier()
            self.stacks.pop()
            st.close()

    def _wait_tok(self, e, tok):
        if tok is None:
            return
        eng = self.engs[e]
        if tok[0] == "e":
            _, f, idx = tok
            if f == e and e == "pe":
                return
            if self.known[e][f] >= idx:
                return
            ep = (idx - 1) // EPOCH
            eng.wait_ge(self.esem(f, ep), (idx - 1) % EPOCH + 1)
            self.known[e][f] = idx
            self.nwaits += 1
        else:
            _, s, val = tok
            if self.known_dma[e][s] >= val:
                return
            eng.wait_ge(self.dma_sems[s], val)
            self.known_dma[e][s] = val
            self.nwaits += 1

    def _deps(self, e, reads, writes):
        for b in reads:
            self._wait_tok(e, b.lw)
        for b in writes:
            self._wait_tok(e, b.lw)
            for tok in b.rd.values():
                self._wait_tok(e, tok)

    def _mark(self, tok, reads, writes):
        key = tok[1] if tok[0] == "e" else ("d", tok[1])
        for b in reads:
            b.rd[key] = tok
        for b in writes:
            b.lw = tok
            b.rd = {}

    def op(self, e, fn, reads=(), writes=()):
        self._deps(e, reads, writes)
        ins = fn(self.engs[e])
        self.cnt[e] += 1
        idx = self.cnt[e]
        ep = (idx - 1) // EPOCH
        assert ep < NSEM_PER_ENG, "instruction budget exceeded on " + e
        ins.then_inc(self.esem(e, ep), 1)
        self._mark(("e", e, idx), reads, writes)
        self.ninstr += 1
        return ins

    def _dma_common(self, q, reads, writes, issue):
        self._deps(q, reads, writes)
        s = self.dma_next
        self.dma_next = (self.dma_next + 1) % NDMA_SEM
        if self.dma_val[s] > 0:
            self._wait_tok(q, ("d", s, self.dma_val[s]))
        self.dma_val[s] += 16
        ins = issue(self.engs[q])
        ins.then_inc(self.dma_sems[s], 16)
        self._mark(("d", s, self.dma_val[s]), reads, writes)
        self.ninstr += 1
        return ins

    def dma(self, q, out, in_, reads=(), writes=(), **kw):
        return self._dma_common(q, reads, writes, lambda e: e.dma_start(out=out, in_=in_, **kw))

    def idma(self, out, in_, off_ap, reads=(), writes=()):
        return self._dma_common(
            "pool", reads, writes,
            lambda e: e.indirect_dma_start(out=out, out_offset=None, in_=in_,
                                           in_offset=bass.IndirectOffsetOnAxis(ap=off_ap, axis=0)))

    def barrier(self):
        for e in self.engs:
            for f in self.engs:
                if f != e and self.cnt[f] > 0:
                    self._wait_tok(e, ("e", f, self.cnt[f]))
            for s in range(NDMA_SEM):
                if self.dma_val[s] > 0:
                    self._wait_tok(e, ("d", s, self.dma_val[s]))

    def mm(self, out, lhsT, rhs, start, stop, r, w):
        return self.op("pe", lambda e: e.matmul(out=out, lhsT=lhsT, rhs=rhs, start=start, stop=stop), r, w)

    def tr(self, out, in_, ident, r, w):
        return self.op("pe", lambda e: e.transpose(out=out, in_=in_, identity=ident), r, w)

    def tt(self, eng, out, in0, in1, op, r, w):
        return self.op(eng, lambda e: e.tensor_tensor(out=out, in0=in0, in1=in1, op=op), r, w)

    def ts(self, eng, out, in0, s1, op0, r, w, s2=None, op1=None):
        if op1 is None:
            return self.op(eng, lambda e: e.tensor_scalar(out=out, in0=in0, scalar1=s1, scalar2=None, op0=op0), r, w)
        return self.op(eng, lambda e: e.tensor_scalar(out=out, in0=in0, scalar1=s1, scalar2=s2, op0=op0, op1=op1), r, w)

    def stt(self, out, in0, scalar, in1, op0, op1, r, w, accum=None):
        return self.op("dve", lambda e: e.scalar_tensor_tensor(out=out, in0=in0, scalar=scalar, in1=in1,
                                                               op0=op0, op1=op1, accum_out=accum), r, w)

    def actv(self, out, in_, func, r, w, bias=None, scale=None, accum=None):
        kw = {}
        if bias is not None:
            kw["bias"] = bias
        if scale is not None:
            kw["scale"] = scale
        if accum is not None:
            kw["accum_out"] = accum
        return self.op("act", lambda e: e.activation(out=out, in_=in_, func=func, **kw), r, w)

    def cp(self, eng, out, in_, r, w):
        if eng == "act":
            return self.op("act", lambda e: e.copy(out=out, in_=in_), r, w)
        return self.op(eng, lambda e: e.tensor_copy(out=out, in_=in_), r, w)

    def memset(self, eng, ap, val, w):
        return self.op(eng, lambda e: e.memset(ap, val), (), w)


D = 1024
NT = 34
UTW = 4384
NEG = -30000.0
ZO, XO, DTO, PLO, QO, KO, VO = 0, 1024, 2560, 2592, 3104, 3616, 4128


def tcol(i):
    return 8 + i * 128 if i < 2 else 280 + (i - 2) * 128


def make_consts():
    k = np.arange(128)
    sec = {}
    cols = []
    off = [0]

    def add(name, arr):
        a = np.zeros((128, arr.shape[1]), np.float32)
        a[:arr.shape[0]] = arr
        sec[name] = (off[0], arr.shape[1])
        off[0] += arr.shape[1]
        cols.append(a)

    add("ident", np.eye(128))
    add("ones", np.ones((128, 128)))
    add("triF", (k[:, None] <= k[None, :]).astype(np.float32))
    add("triB", (k[:, None] >= k[None, :]).astype(np.float32))
    nmF = np.where(k[None, :] < k[:, None], NEG, 0.0)
    nmB = np.where(k[None, :] > k[:, None], NEG, 0.0)
    add("nmF", np.tile(nmF, (1, 4)))
    add("nmB", np.tile(nmB, (1, 4)))
    add("iota", np.tile(np.arange(256)[None, :], (128, 1)).astype(np.float32))
    q = np.arange(64)
    cs = np.clip(q - 8, 0, 48)
    kc = np.arange(64)
    valid = (kc[:, None] >= cs[None, :]) & (kc[:, None] < cs[None, :] + 16)
    add("cmask", np.where(valid, 0.0, NEG))
    add("J64", np.eye(64)[::-1].copy())
    MA = np.zeros((128, 3, 4, 128), np.float32)
    MB = np.zeros((16, 3, 4, 128), np.float32)
    t = np.arange(128)
    for v in range(3):
        for gi, w in enumerate((2, 4, 8, 16)):
            lo = t - w // 2
            hi = t + w - w // 2 - 1
            if v == 1:
                lo = np.maximum(lo, 0)
            if v == 2:
                hi = np.minimum(hi, 127)
            cnt = (hi - lo + 1).astype(np.float32)
            for part, M, s_abs in ((0, MA, np.arange(128) - 8), (1, MB, 120 + np.arange(16))):
                inw = (s_abs[:, None] >= lo[None, :]) & (s_abs[:, None] <= hi[None, :])
                M[:, v, gi, :] = inw / cnt[None, :] - (s_abs[:, None] == t[None, :])
    ncom = off[0]
    add("MA", MA.reshape(128, -1))
    add("MB", MB.reshape(16, -1))
    return np.concatenate(cols, axis=1), sec, ncom


CST, SEC, NCOM = make_consts()
NCST = CST.shape[1]

WSPEC = [("ada_w", [2, 1024, 6144]), ("ada_b", [2, 6144]), ("norm1_g", [2, 1024]), ("w_in", [2, 1024, 4640]),
         ("conv_w", [2, 3, 1536]), ("conv_b", [2, 1536]), ("a_log", [2, 2, 16]), ("dt_bias", [2, 2, 16]),
         ("d_skip", [2, 16]), ("ssd_norm_g", [2, 1024]), ("pool_w", [2, 4, 128, 128]), ("pool_scale", [2, 512]),
         ("na_rpb", [2, 8, 15, 31]), ("w_out", [2, 2048, 1024]), ("norm2_g", [2, 1024]),
         ("peer_wq", [2, 1024, 2048]), ("peer_keys", [2, 8, 2, 128, 128]), ("peer_u", [2, 16384, 1024]),
         ("peer_v", [2, 16384, 1024]), ("final_g", [1024])]


def build_program(nlayers=2, dbg=None):
    nc = bass.Bass("TRN2", target_bir_lowering=False)

    def din(name, shape, dt=F32):
        return nc.dram_tensor(name, list(shape), dt, kind="ExternalInput").ap()

    def dscr(name, shape, dt=F32):
        return nc.dram_tensor(name, list(shape), dt, kind="Internal").ap()

    x_in = din("x", [4096, D])
    ctx_in = din("ctx", [256, D])
    c2_in = din("c2", [2, D])
    cst_in = din("cst", [128, NCST])
    W = {n: din(n, s) for n, s in WSPEC}
    y_out = nc.dram_tensor("y", [4096, D], F32, kind="ExternalOutput").ap()
    dbg_out = {}
    if dbg:
        for n, s in dbg.items():
            if n.startswith("_"):
                continue
            dbg_out[n] = nc.dram_tensor(n, list(s), F32, kind="ExternalOutput").ap()

    hS = dscr("hS", [NT * 128, D])
    mixS = dscr("mixS", [NT * 128, 2048], BF16)
    stS = dscr("stS", [2, NT, 128, 1024], BF16)
    csbS = dscr("csbS", [NT, 128, 1024])
    KTs = dscr("KTs", [64, 8, 4096], BF16)
    Vs = dscr("Vs", [64, 64, 520], BF16)
    rpbP = dscr("rpbP", [120, 160])
    gS = dscr("gS", [128, 4096])
    gSb = Buf("gS")
    UV = dscr("UV", [32768, 2048], BF16)
    UVb = Buf("UV")

    hB = [Buf(f"h{i}") for i in range(NT)]
    mixB = [Buf(f"mix{i}") for i in range(NT)]
    stB = [[Buf(f"st{d}_{i}") for i in range(NT)] for d in range(2)]
    csbB = [Buf(f"csb{i}") for i in range(NT)]
    ktB = [Buf(f"kt{i}") for i in range(32)]
    vB = [Buf(f"v{i}") for i in range(32)]
    rpbB = Buf("rpbP")
    yB = Buf("y")
    dbgB = Buf("dbg")

    with ExitStack() as st0:
        P = Prog(nc, st0)
        cst, cstb = P.sb("cst", [128, NCOM])
        P.dma("sp", cst[:], cst_in[:, 0:NCOM], writes=[cstb])

        def C(name, rows=128):
            o, w = SEC[name]
            return cst[0:rows, o:o + w]

        identf = C("ident")
        ones = C("ones")
        identb, identbb = P.sb("identb", [128, 128], BF16)
        P.cp("dve", identb[:], identf, [cstb], [identbb])
        zero, zerob = P.sb("zero", [128, 160])
        P.memset("pool", zero[:], 0.0, [zerob])

        with P.scope():
            pk = [P.sb(f"pk{n}", [128, 8, 1024]) for n in range(3)]
            pk16 = [P.sb(f"pkh{n}", [128, 8, 1024], BF16) for n in range(3)]
            cnt = 0
            for tname, off in (("peer_u", 0), ("peer_v", 1024)):
                flat = W[tname].rearrange("l e d -> (l e) d")
                for c in range(32):
                    pt, ptb = pk[cnt % 3]
                    ph, phb = pk16[cnt % 3]
                    ceng = ("dve", "act", "pool")[cnt % 3]
                    cnt += 1
                    P.dma("sp", pt[:], flat[c * 1024:(c + 1) * 1024, :].rearrange("(p r) d -> p r d", p=128), writes=[ptb])
                    P.cp(ceng, ph[:], pt[:], [ptb], [phb])
                    P.dma("sp", UV[c * 1024:(c + 1) * 1024, off:off + 1024].rearrange("(p r) d -> p r d", p=128), ph[:],
                          reads=[phb], writes=[UVb])

        for l in range(nlayers):
            last = l == nlayers - 1
            tiles_all = list(range(NT))
            out_tiles = list(range(2, NT)) if last else list(range(NT))
            with P.scope():
                n1g, n1gb = P.sb("n1g", [128, 8])
                n2g, n2gb = P.sb("n2g", [128, 8])
                adab, adabb = P.sb("adab", [128, 48])
                scT, scTb = P.sb("scT", [128, 8, 2])
                modT, modTb = P.sb("modT", [128, 48, 2])
                A1, A1b = P.sb("A1", [128, 8, 2])
                A2, A2b = P.sb("A2", [128, 8, 2])
                slow = dict(allow_slow_non_contiguous=True)
                P.dma("sp", n1g[:], W["norm1_g"][l].rearrange("(k p) -> p k", p=128), writes=[n1gb], **slow)
                P.dma("sp", n2g[:], W["norm2_g"][l].rearrange("(k p) -> p k", p=128), writes=[n2gb], **slow)
                P.dma("sp", adab[:], W["ada_b"][l].rearrange("(k p) -> p k", p=128), writes=[adabb], **slow)
                for r_ in range(2):
                    P.dma("sp", scT[:, :, r_], c2_in[r_].rearrange("(k p) -> p k", p=128), writes=[scTb], **slow)
                P.actv(scT[:], scT[:], AF.Silu, [scTb], [scTb])
                with P.scope():
                    gbc, gbcb = P.sb("gbc", [128, 2, 2, 1024])
                    scB, scBb = P.sb("scB", [128, 8, 2, 128])
                    P.cp("dve", scB[:], scT[:].unsqueeze(3).to_broadcast([128, 8, 2, 128]), [scTb], [scBb])
                    adr, adrb = P.sb("adr", [128, 2, 1024])
                    P.dma("sp", adr[:, 0, :], W["ada_b"][l, 2048:3072].partition_broadcast(128), writes=[adrb])
                    P.dma("sp", adr[:, 1, :], W["ada_b"][l, 5120:6144].partition_broadcast(128), writes=[adrb])
                    aw, awb = P.sb("aw", [128, 8, 512])
                    modps, modpsb = P.ps("modps", [128, 48, 2])
                    gps, gpsb = P.ps("gps", [128, 512])
                    for pc in range(12):
                        P.dma("sp", aw[:], W["ada_w"][l, :, pc * 512:(pc + 1) * 512].rearrange("(k p) n -> p k n", p=128),
                              writes=[awb])
                        for n4 in range(4):
                            j = pc * 4 + n4
                            for k in range(8):
                                P.mm(modps[:, j, :], aw[:, k, n4 * 128:(n4 + 1) * 128], scT[:, k, :], k == 0, k == 7,
                                     [awb, scTb], [modpsb])
                        if pc in (4, 5, 10, 11):
                            which = 0 if pc < 6 else 1
                            half = pc % 2
                            for r in range(2):
                                for k in range(8):
                                    P.mm(gps[:], scB[:, k, r, :], aw[:, k, :], k == 0, k == 7, [scBb, awb], [gpsb])
                                P.tt("dve", gbc[:, which, r, half * 512:(half + 1) * 512], gps[:],
                                     adr[:, which, half * 512:(half + 1) * 512], ALU.add, [gpsb, adrb], [gbcb])
                    P.dma("sp", gS[:, :], gbc[:].rearrange("p a b c -> p (a b c)"), reads=[gbcb], writes=[gSb])
                    P.tt("dve", modT[:], modps[:], adab[:].unsqueeze(2).to_broadcast([128, 48, 2]), ALU.add,
                         [modpsb, adabb], [modTb])
                    P.ts("dve", A1[:], modT[:, 8:16, :], 1.0, ALU.add, [modTb], [A1b])
                    P.tt("dve", A1[:], A1[:], n1g[:].unsqueeze(2).to_broadcast([128, 8, 2]), ALU.mult, [A1b, n1gb], [A1b])
                    P.ts("dve", A2[:], modT[:, 32:40, :], 1.0, ALU.add, [modTb], [A2b])
                    P.tt("dve", A2[:], A2[:], n2g[:].unsqueeze(2).to_broadcast([128, 8, 2]), ALU.mult, [A2b, n2gb], [A2b])
                B1 = modT[:, 0:8, :]
                B2 = modT[:, 24:32, :]
                mix_scope = P.scope()
                mix_scope.__enter__()
                uT, uTb = P.sb("uT", [128, 8, UTW], BF16)
                P.memset("pool", uT[:], 0.0, [uTb])
                uTa = uTd = uTb

                with P.scope():
                    ht, htb = P.sb("ht", [128, D])
                    junk, junkb = P.sb("junk", [128, D])
                    ss, ssb = P.sb("ss", [128, 1])
                    xn, xnb = P.sb("xn", [128, D], BF16)
                    tp8, tp8b = P.ps("tp8", [128, 8, 128], BF16)
                    for i in tiles_all:
                        if l == 0:
                            src = ctx_in[i * 128:(i + 1) * 128, :] if i < 2 else x_in[(i - 2) * 128:(i - 1) * 128, :]
                            rds = []
                        else:
                            src = hS[i * 128:(i + 1) * 128, :]
                            rds = [hB[i]]
                        r = 1 if i < 2 else 0
                        col = tcol(i)
                        P.dma("sp", ht[:], src, reads=rds, writes=[htb])
                        P.actv(junk[:], ht[:], AF.Square, [htb], [junkb, ssb], accum=ss[:])
                        P.actv(ss[:], ss[:], AF.Sqrt, [ssb], [ssb], bias=1e-6, scale=1.0 / D)
                        P.op("dve", lambda e: e.reciprocal(out=ss[:], in_=ss[:]), [ssb], [ssb])
                        P.ts("dve", xn[:], ht[:], ss[:, 0:1], ALU.mult, [htb, ssb], [xnb])
                        for k in range(8):
                            P.tr(tp8[:, k, :], xn[:, k * 128:(k + 1) * 128], identb[:], [xnb, identbb], [tp8b])
                        for k in range(8):
                            if k < 4:
                                P.actv(uT[:, k, col:col + 128], tp8[:, k, :], AF.Identity, [tp8b, A1b, modTb], [uTa],
                                       bias=B1[:, k, r:r + 1], scale=A1[:, k, r:r + 1])
                            else:
                                P.ts("dve", uT[:, k, col:col + 128], tp8[:, k, :], A1[:, k, r:r + 1], ALU.mult,
                                     [tp8b, A1b, modTb], [uTd], s2=B1[:, k, r:r + 1], op1=ALU.add)

                uTr = [uTb]
                with P.scope():
                    winA, winAb = P.sb("winA", [128, 8, 2592], BF16)
                    for k in range(8):
                        for c0 in range(0, 2592, 864):
                            P.dma("pool", winA[:, k, c0:c0 + 864], W["w_in"][l, k * 128:(k + 1) * 128, c0:c0 + 864],
                                  writes=[winAb])
                    cw, cwb = P.sb("cw", [128, 12, 3])
                    cb, cbb = P.sb("cb", [128, 12])
                    for t_ in range(3):
                        P.dma("sp", cw[:, :, t_], W["conv_w"][l, t_].rearrange("(j p) -> p j", p=128), writes=[cwb], **slow)
                    P.dma("sp", cb[:], W["conv_b"][l].rearrange("(j p) -> p j", p=128), writes=[cbb], **slow)
                    dtb, dtbb = P.sb("dtb", [128, 32])
                    Arow, Arowb = P.sb("Arow", [128, 32])
                    Drow, Drowb = P.sb("Drow", [128, 16])
                    ngbc, ngbcb = P.sb("ngbc", [128, D])
                    P.dma("sp", dtb[:], W["dt_bias"][l].rearrange("a b -> (a b)").partition_broadcast(128), writes=[dtbb])
                    P.dma("sp", Arow[:], W["a_log"][l].rearrange("a b -> (a b)").partition_broadcast(128), writes=[Arowb])
                    P.dma("sp", Drow[:], W["d_skip"][l].partition_broadcast(128), writes=[Drowb])
                    P.dma("sp", ngbc[:], W["ssd_norm_g"][l].partition_broadcast(128), writes=[ngbcb])
                    P.actv(Arow[:], Arow[:], AF.Exp, [Arowb], [Arowb])
                    P.ts("dve", Arow[:], Arow[:], -1.0, ALU.mult, [Arowb], [Arowb])

                    psA, psAb = P.ps("psA", [128, 3, 130])
                    tpp, tppb = P.ps("tpp", [128, 8, 128], BF16)
                    small, smallb = P.ps("small", [128, 512])
                    csrow, csrowb = P.ps("csrow", [128, 512])
                    big, bigb = P.ps("big", [128, 1024])
                    yps, ypsb = P.ps("yps", [128, 1024])

                    xcT, xcTb = P.sb("xcT", [128, 12, 128], BF16)
                    tcv, tcvb = P.sb("tcv", [128, 128])
                    xs, xsb = P.sb("xs", [128, 1024], BF16)
                    Btm, Btmb = P.sb("Btm", [128, 256], BF16)
                    dtx, dtxb = P.sb("dtx", [128, 32])
                    dt, dtb_ = P.sb("dt", [128, 32])
                    av, avb = P.sb("av", [128, 32])
                    cscol, cscolb = P.sb("cscol", [128, 32])
                    negcs, negcsb = P.sb("negcs", [128, 32])
                    dd, ddb = P.sb("dd", [128, 32])
                    dte, dteb = P.sb("dte", [128, 32])
                    ein, einb = P.sb("ein", [128, 32])
                    dtot, dtotb = P.sb("dtot", [128, 32])
                    wA, wAb = P.sb("wA", [128, 32])
                    xd0, xd0b = P.sb("xd0", [128, 1024], BF16)
                    xd1, xd1b = P.sb("xd1", [128, 1024], BF16)
                    xd = [(xd0, xd0b), (xd1, xd1b)]
                    Sst, Sstb = P.sb("Sst", [128, 1024])
                    Rst, Rstb = P.sb("Rst", [128, 1024])
                    sbf, sbfb = P.sb("sbf", [128, 1024], BF16)
                    t0, t0b = P.sb("t0", [128, 1024])
                    csb, csbb = t0, t0b
                    dtotB, dtotBb = P.sb("dtotB", [128, NT, 16])
                    P.memset("dve", Sst[:], 0.0, [Sstb])
                    P.memset("dve", Rst[:], 0.0, [Rstb])

                    def bc16(ap16):
                        return ap16.unsqueeze(2).to_broadcast([128, 16, 64])

                    def v3(t):
                        return t.rearrange("p (h d) -> p h d", h=16)

                    def front(i, passB):
                        col = tcol(i)
                        nj = 12 if passB else 10
                        for j0 in range(0, nj, 3):
                            js = list(range(j0, min(j0 + 3, nj)))
                            for jj, j in enumerate(js):
                                for k in range(8):
                                    P.mm(psA[:, jj, :], winA[:, k, XO + j * 128:XO + (j + 1) * 128],
                                         uT[:, k, col - 1:col + 129], k == 0, k == 7, [winAb] + uTr, [psAb])
                            for jj, j in enumerate(js):
                                P.ts("dve", tcv[:], psA[:, jj, 0:128], cw[:, j, 0:1], ALU.mult, [psAb, cwb], [tcvb])
                                P.stt(tcv[:], psA[:, jj, 1:129], cw[:, j, 1:2], tcv[:], ALU.mult, ALU.add,
                                      [psAb, cwb, tcvb], [tcvb])
                                P.stt(tcv[:], psA[:, jj, 2:130], cw[:, j, 2:3], tcv[:], ALU.mult, ALU.add,
                                      [psAb, cwb, tcvb], [tcvb])
                                P.actv(xcT[:, j, :], tcv[:], AF.Silu, [tcvb, cbb], [xcTb], bias=cb[:, j:j + 1])
                        for j in range(8):
                            P.tr(tpp[:, j, :], xcT[:, j, :], identb[:], [xcTb, identbb], [tppb])
                        P.cp("act", xs[:], tpp[:].rearrange("p a b -> p (a b)"), [tppb], [xsb])
                        for j in range(2):
                            P.tr(tpp[:, j, :], xcT[:, 8 + j, :], identb[:], [xcTb, identbb], [tppb])
                        P.cp("act", Btm[:], tpp[:, 0:2, :].rearrange("p a b -> p (a b)"), [tppb], [Btmb])
                        for k in range(8):
                            P.mm(small[:, 0:32], uT[:, k, col:col + 128], winA[:, k, DTO:DTO + 32], k == 0, k == 7,
                                 [winAb] + uTr, [smallb])
                        P.tt("dve", dtx[:], small[:, 0:32], dtb[:], ALU.add, [smallb, dtbb], [dtxb])
                        P.actv(dtx[:], dtx[:], AF.Exp, [dtxb], [dtxb])
                        P.actv(dt[:], dtx[:], AF.Ln, [dtxb], [dtb_], bias=1.0)
                        P.tt("dve", av[:], dt[:], Arow[:], ALU.mult, [dtb_, Arowb], [avb])
                        P.mm(small[:, 32:48], C("triF"), av[:, 0:16], True, True, [cstb, avb], [smallb])
                        P.mm(small[:, 48:64], C("triB"), av[:, 16:32], True, True, [cstb, avb], [smallb])
                        P.mm(small[:, 64:96], ones, av[:, 0:32], True, True, [cstb, avb], [smallb])
                        P.cp("dve", cscol[:], small[:, 32:64], [smallb], [cscolb])
                        P.ts("dve", negcs[:], cscol[:], -1.0, ALU.mult, [cscolb], [negcsb])
                        P.tt("dve", dd[:], small[:, 64:96], cscol[:], ALU.subtract, [smallb, cscolb], [ddb])
                        P.actv(dte[:], dd[:], AF.Exp, [ddb], [dteb])
                        P.actv(ein[:], cscol[:], AF.Exp, [cscolb], [einb])
                        P.actv(dtot[:], small[:, 64:96], AF.Exp, [smallb], [dtotb])

                    def passA(i):
                        front(i, False)
                        P.tt("dve", wA[:], dt[:], dte[:], ALU.mult, [dtb_, dteb], [wAb])
                        for d in range(2):
                            xdt, xdb = xd[d]
                            P.tt("dve", v3(xdt[:]), v3(xs[:]), bc16(wA[:, d * 16:(d + 1) * 16]), ALU.mult,
                                 [xsb, wAb], [xdb])
                        P.cp("act", sbf[:], Sst[:], [Sstb], [sbfb])
                        P.dma("pool", stS[0, i], sbf[:], reads=[sbfb], writes=[stB[0][i]])
                        for g in range(2):
                            P.mm(big[:, g * 512:(g + 1) * 512], Btm[:, g * 128:(g + 1) * 128],
                                 xd0[:, g * 512:(g + 1) * 512], True, True, [Btmb, xd0b], [bigb])
                        P.tt("dve", v3(Sst[:]), v3(Sst[:]), bc16(dtot[:, 0:16]), ALU.mult, [Sstb, dtotb], [Sstb])
                        P.tt("dve", Sst[:], big[:], Sst[:], ALU.add, [bigb, Sstb], [Sstb])
                        for g in range(2):
                            P.mm(big[:, g * 512:(g + 1) * 512], Btm[:, g * 128:(g + 1) * 128],
                                 xd1[:, g * 512:(g + 1) * 512], True, True, [Btmb, xd1b], [bigb])
                        P.cp("act", csb[:], big[:], [bigb], [csbb])
                        P.dma("pool", csbS[i], csb[:], reads=[csbb], writes=[csbB[i]])
                        P.cp("dve", dtotB[:, i, :], dtot[:, 16:32], [dtotb], [dtotBb])

                    def rsweep(tl):
                        for i in reversed(tl):
                            P.cp("act", sbf[:], Rst[:], [Rstb], [sbfb])
                            P.dma("pool", stS[1, i], sbf[:], reads=[sbfb], writes=[stB[1][i]])
                            P.dma("sp", csb[:], csbS[i], reads=[csbB[i]], writes=[csbb])
                            P.tt("dve", v3(Rst[:]), v3(Rst[:]), bc16(dtotB[:, i, :]), ALU.mult, [Rstb, dtotBb], [Rstb])
                            P.tt("dve", Rst[:], csb[:], Rst[:], ALU.add, [csbb, Rstb], [Rstb])

                    cbT, cbTb = P.sb("cbT", [128, 256])
                    Ua, Uab = P.sb("Ua", [128, 4, 128])
                    LT, LTb = P.sb("LT", [128, 128])
                    MT, MTb = P.sb("MT", [128, 32, 128], BF16)
                    Sl0, Sl0b = P.sb("Sl0", [128, 1024], BF16)
                    Sl1, Sl1b = P.sb("Sl1", [128, 1024], BF16)
                    Sl = [(Sl0, Sl0b), (Sl1, Sl1b)]
                    t1, t1b = P.sb("t1", [128, 1024])
                    ssq, ssqb = P.sb("ssq", [128, 1])
                    yn, ynb = P.sb("yn", [128, 1024], BF16)

                    def passB(i):
                        front(i, True)
                        col = tcol(i)
                        for d in range(2):
                            P.dma("sp", Sl[d][0][:], stS[d, i], reads=[stB[d][i]], writes=[Sl[d][1]])
                        for g in range(2):
                            P.mm(small[:, 128 + g * 128:256 + g * 128], xcT[:, 8 + g, :], xcT[:, 10 + g, :], True, True,
                                 [xcTb], [smallb])
                        P.cp("dve", cbT[:], small[:, 128:384], [smallb], [cbTb])
                        for d in range(2):
                            xdt, xdb = xd[d]
                            P.tt("dve", v3(xdt[:]), v3(xs[:]), bc16(dt[:, d * 16:(d + 1) * 16]), ALU.mult,
                                 [xsb, dtb_], [xdb])
                        for d in range(2):
                            tri = C("triF") if d == 0 else C("triB")
                            nm = C("nmF") if d == 0 else C("nmB")
                            for hb in range(4):
                                P.tt("dve", Ua[:], av[:, d * 16 + hb * 4:d * 16 + hb * 4 + 4].unsqueeze(2).to_broadcast([128, 4, 128]),
                                     tri.unsqueeze(1).to_broadcast([128, 4, 128]), ALU.mult, [avb, cstb], [Uab])
                                P.mm(csrow[:], ones, Ua[:].rearrange("p a b -> p (a b)"), True, False, [cstb, Uab], [csrowb])
                                P.mm(csrow[:], identf, nm, False, True, [cstb], [csrowb])
                                for hh in range(4):
                                    h = hb * 4 + hh
                                    P.actv(LT[:], csrow[:, hh * 128:(hh + 1) * 128], AF.Exp, [csrowb, negcsb], [LTb],
                                           bias=negcs[:, d * 16 + h:d * 16 + h + 1])
                                    g = h // 8
                                    P.tt("dve", MT[:, d * 16 + h, :], LT[:], cbT[:, g * 128:(g + 1) * 128], ALU.mult,
                                         [LTb, cbTb], [MTb])
                        for h in range(16):
                            P.mm(yps[:, h * 64:(h + 1) * 64], MT[:, h, :], xd0[:, h * 64:(h + 1) * 64], True, False,
                                 [MTb, xd0b], [ypsb])
                            P.mm(yps[:, h * 64:(h + 1) * 64], MT[:, 16 + h, :], xd1[:, h * 64:(h + 1) * 64], False, True,
                                 [MTb, xd1b], [ypsb])
                        for d in range(2):
                            Sd, Sdb = Sl[d]
                            for g in range(2):
                                P.mm(big[:, g * 512:(g + 1) * 512], xcT[:, 10 + g, :], Sd[:, g * 512:(g + 1) * 512],
                                     True, True, [xcTb, Sdb], [bigb])
                            tdst, tdstb = (t0, t0b) if d == 0 else (t1, t1b)
                            P.tt("dve", v3(tdst[:]), v3(big[:]), bc16(ein[:, d * 16:(d + 1) * 16]), ALU.mult,
                                 [bigb, einb], [tdstb])
                        P.tt("dve", t0[:], t0[:], t1[:], ALU.add, [t0b, t1b], [t0b])
                        P.tt("dve", t0[:], yps[:], t0[:], ALU.add, [ypsb, t0b], [t0b])
                        P.tt("dve", v3(t1[:]), v3(xs[:]), bc16(Drow[:]), ALU.mult, [xsb, Drowb], [t1b])
                        P.tt("dve", t0[:], t0[:], t1[:], ALU.add, [t0b, t1b], [t0b])
                        for n in range(2):
                            for k in range(8):
                                P.mm(big[:, n * 512:(n + 1) * 512], uT[:, k, col:col + 128],
                                     winA[:, k, ZO + n * 512:ZO + (n + 1) * 512], k == 0, k == 7, [winAb] + uTr, [bigb])
                        P.actv(t1[:], big[:], AF.Silu, [bigb], [t1b])
                        P.tt("dve", t0[:], t0[:], t1[:], ALU.mult, [t0b, t1b], [t0b])
                        P.actv(t1[:], t0[:], AF.Square, [t0b], [t1b, ssqb], accum=ssq[:])
                        P.actv(ssq[:], ssq[:], AF.Sqrt, [ssqb], [ssqb], bias=1e-6, scale=1.0 / D)
                        P.op("dve", lambda e: e.reciprocal(out=ssq[:], in_=ssq[:]), [ssqb], [ssqb])
                        P.stt(yn[:], t0[:], ssq[:, 0:1], ngbc[:], ALU.mult, ALU.mult, [t0b, ssqb, ngbcb], [ynb])
                        P.dma("pool", mixS[i * 128:(i + 1) * 128, 0:1024], yn[:], reads=[ynb], writes=[mixB[i]])

                    for i in (0, 1):
                        passA(i)
                    rsweep([0, 1])
                    if not last:
                        for i in (0, 1):
                            passB(i)
                    for i in range(2, NT):
                        passA(i)
                    rsweep(list(range(2, NT)))
                    for i in range(2, NT):
                        passB(i)

                with P.scope():
                    winB, winBb = P.sb("winB", [128, 8, 2048], BF16)
                    for k in range(8):
                        for c0 in range(0, 2048, 1024):
                            P.dma("pool", winB[:, k, c0:c0 + 1024],
                                  W["w_in"][l, k * 128:(k + 1) * 128, PLO + c0:PLO + c0 + 1024], writes=[winBb])
                    wpl, wplb = P.sb("wpl", [128, 4, 128], BF16)
                    P.dma("pool", wpl[:], W["pool_w"][l].rearrange("g i o -> i g o"), writes=[wplb])
                    psc, pscb = P.sb("psc", [128, 512])
                    P.dma("sp", psc[:], W["pool_scale"][l].partition_broadcast(128), writes=[pscb])
                    ptab, ptabb = P.sb("ptab", [128, NCST - NCOM])
                    P.dma("sp", ptab[:], cst_in[:, NCOM:NCST], writes=[ptabb])
                    oMA = SEC["MA"][0] - NCOM
                    oMB = SEC["MB"][0] - NCOM
                    MA = ptab[:, oMA:oMA + 1536].rearrange("p (v g t) -> p v g t", v=3, g=4)
                    MB = ptab[0:16, oMB:oMB + 1536].rearrange("p (v g t) -> p v g t", v=3, g=4)

                    with P.scope():
                        plAp, plApb = P.ps("plAp", [128, 512])
                        plBp, plBpb = P.ps("plBp", [16, 512])
                        ppp, pppb = P.ps("ppp", [128, 512])
                        ypp, yppb = P.ps("ypp", [128, 512])
                        plA, plAb = P.sb("plA", [128, 512])
                        plB, plBb = P.sb("plB", [16, 512])
                        pT, pTb = P.sb("pT", [128, 512], BF16)
                        yo, yob = P.sb("yo", [128, 512], BF16)
                        for i in out_tiles:
                            col = tcol(i)
                            v = 1 if i in (0, 2) else (2 if i in (1, NT - 1) else 0)
                            for k in range(8):
                                P.mm(plAp[:], uT[:, k, col - 8:col + 120], winB[:, k, 0:512], k == 0, k == 7,
                                     [winBb] + uTr, [plApb])
                            for k in range(8):
                                P.mm(plBp[:], uT[:, k, col + 120:col + 136], winB[:, k, 0:512], k == 0, k == 7,
                                     [winBb] + uTr, [plBpb])
                            P.cp("act", plA[:], plAp[:], [plApb], [plAb])
                            P.cp("dve", plB[:], plBp[:], [plBpb], [plBb])
                            for g in range(4):
                                P.mm(ppp[:, g * 128:(g + 1) * 128], plA[:, g * 128:(g + 1) * 128], MA[:, v, g, :], True, False,
                                     [plAb, ptabb], [pppb])
                                P.mm(ppp[:, g * 128:(g + 1) * 128], plB[:, g * 128:(g + 1) * 128], MB[:, v, g, :], False, True,
                                     [plBb, ptabb], [pppb])
                            P.cp("act", pT[:], ppp[:], [pppb], [pTb])
                            for g in range(4):
                                P.mm(ypp[:, g * 128:(g + 1) * 128], pT[:, g * 128:(g + 1) * 128], wpl[:, g, :], True, True,
                                     [pTb, wplb], [yppb])
                            P.tt("dve", yo[:], ypp[:], psc[:], ALU.mult, [yppb, pscb], [yob])
                            P.dma("pool", mixS[i * 128:(i + 1) * 128, 1024:1536], yo[:], reads=[yob], writes=[mixB[i]])

                    with P.scope():
                        QW, KW, VW = QO - PLO, KO - PLO, VO - PLO
                        BT, BTb = P.sb("BT", [64, 120, 64], BF16)
                        ident64 = identb[0:64, 0:64]
                        with P.scope():
                            Hf, Hfb = P.sb("Hf", [64, 120, 64])
                            P.dma("sp", rpbP[:, :], zero[0:120, :], reads=[zerob], writes=[rpbB])
                            rev = bass.AP(W["na_rpb"].tensor, l * 120 * 31 + 30, [[31, 120], [-1, 31]])
                            P.dma("sp", rpbP[:, 48:79], rev, writes=[rpbB], allow_slow_non_contiguous=True)
                            hank = bass.AP(rpbP.tensor, 0, [[1, 64], [160, 120], [1, 64]])
                            P.dma("sp", Hf[:], hank, reads=[rpbB], writes=[Hfb])
                            btp, btpb = P.ps("btp", [64, 8, 64])
                            for bi in range(15):
                                P.mm(btp[:], C("J64", 64), Hf[:, bi * 8:(bi + 1) * 8, :], True, True, [cstb, Hfb], [btpb])
                                P.tt("dve", BT[:, bi * 8:(bi + 1) * 8, :], btp[:],
                                     C("cmask", 64).unsqueeze(1).to_broadcast([64, 8, 64]), ALU.add, [btpb, cstb], [BTb])
                        kps, kpsb = P.ps("kps", [64, 8, 128])
                        vps, vpsb = P.ps("vps", [64, 2, 512])
                        sc, scb = P.ps("sc", [64, 1024])
                        o0, o0b = P.ps("o0", [64, 4, 65])
                        o1, o1b = P.ps("o1", [64, 4, 65])
                        KTc, KTcb = P.sb("KTc", [64, 8, 256], BF16)
                        Vc, Vcb = P.sb("Vc", [64, 4, 8, 65], BF16)
                        kt, ktb_ = P.sb("kt", [64, 8, 128], BF16)
                        vt, vtb = P.sb("vt", [64, 2, 8, 65], BF16)
                        QT, QTb = P.sb("QT", [64, 8, 128], BF16)
                        KTw, KTwb = P.sb("KTw", [64, 8, 576], BF16)
                        Vw, Vwb = P.sb("Vw", [64, 9, 520], BF16)
                        PT, PTb = P.sb("PT", [64, 768], BF16)
                        rinv, rinvb = P.sb("rinv", [64, 8])
                        nao, naob = P.sb("nao", [64, 512], BF16)
                        P.memset("dve", vt[:], 1.0, [vtb])
                        P.memset("dve", Vc[:], 1.0, [Vcb])

                        def proj_T(dst, dstb, off, i, scale=None):
                            col = tcol(i)
                            for h in range(8):
                                for k in range(8):
                                    P.mm(kps[:, h, :], winB[:, k, off + h * 64:off + (h + 1) * 64], uT[:, k, col:col + 128],
                                         k == 0, k == 7, [winBb] + uTr, [kpsb])
                            if scale is None:
                                P.cp("act", dst, kps[:], [kpsb], [dstb])
                            else:
                                P.actv(dst, kps[:], AF.Identity, [kpsb], [dstb], scale=scale)

                        def proj_V(dst4, dstb, i):
                            col = tcol(i)
                            for rr in range(2):
                                for k in range(8):
                                    P.mm(vps[:, rr, :], uT[:, k, col + rr * 64:col + (rr + 1) * 64], winB[:, k, VW:VW + 512],
                                         k == 0, k == 7, [winBb] + uTr, [vpsb])
                            P.cp("dve", dst4, vps[:].rearrange("p r (h d) -> p r h d", h=8), [vpsb], [dstb])

                        for i in (0, 1):
                            proj_T(KTc[:, :, i * 128:(i + 1) * 128], KTcb, KW, i)
                            proj_V(Vc[:, 2 * i:2 * i + 2, :, 0:64], Vcb, i)
                        for i in range(2, NT):
                            j = i - 2
                            proj_T(kt[:], ktb_, KW, i)
                            P.dma("pool", KTs[:, :, j * 128:(j + 1) * 128], kt[:], reads=[ktb_], writes=[ktB[j]])
                            proj_V(vt[:, :, :, 0:64], vtb, i)
                            P.dma("pool", Vs[2 * j:2 * j + 2].rearrange("r k e -> k r e"),
                                  vt[:].rearrange("p r h e -> p r (h e)"), reads=[vtb], writes=[vB[j]])

                        def attend(i):
                            isctx = i < 2
                            proj_T(QT[:], QTb, QW, i, scale=0.125)
                            if not isctx:
                                j = i - 2
                                r0s = [min(max(2 * j + rr - 4, 0), 56) for rr in range(2)]
                                rlo, rhi = r0s[0], r0s[1] + 8
                                nr = rhi - rlo
                                tl = sorted(set(r // 2 for r in range(rlo, rhi)))
                                P.dma("sp", KTw[:, :, 0:nr * 64], KTs[:, :, rlo * 64:rhi * 64], reads=[ktB[t] for t in tl],
                                      writes=[KTwb])
                                P.dma("sp", Vw[:, 0:nr, :], Vs[rlo:rhi].rearrange("r k e -> k r e"), reads=[vB[t] for t in tl],
                                      writes=[Vwb])
                            for rr in range(2):
                                qs = QT
                                for h in range(8):
                                    q_ap = QT[:, h, rr * 64:(rr + 1) * 64]
                                    nloc = 0
                                    if not isctx:
                                        r = 2 * (i - 2) + rr
                                        r0 = r0s[rr]
                                        dr0 = r0 - r + 7
                                        nloc = 8
                                        P.mm(sc[:, 0:512], ident64, BT[:, h * 15 + dr0:h * 15 + dr0 + 8, :], True, False,
                                             [identbb, BTb], [scb])
                                        for jj in range(8):
                                            off = (r0 + jj - rlo) * 64
                                            P.mm(sc[:, jj * 64:(jj + 1) * 64], KTw[:, h, off:off + 64], q_ap, False, jj == 7,
                                                 [KTwb, QTb], [scb])
                                    for jc in range(4):
                                        P.mm(sc[:, 512 + jc * 64:512 + (jc + 1) * 64], KTc[:, h, jc * 64:(jc + 1) * 64], q_ap,
                                             True, True, [KTcb, QTb], [scb])
                                    lo = 0 if not isctx else 512
                                    P.actv(PT[:, lo:768], sc[:, lo:768], AF.Exp, [scb], [PTb])
                                    ot, otb = (o0, o0b) if h < 4 else (o1, o1b)
                                    blocks = []
                                    for jj in range(nloc):
                                        blocks.append((PT[:, jj * 64:(jj + 1) * 64],
                                                       Vw[:, r0 + jj - rlo, h * 65:(h + 1) * 65], Vwb))
                                    for jc in range(4):
                                        blocks.append((PT[:, 512 + jc * 64:512 + (jc + 1) * 64], Vc[:, jc, h, :], Vcb))
                                    for bi, (pa, va, vb_) in enumerate(blocks):
                                        P.mm(ot[:, h % 4, :], pa, va, bi == 0, bi == len(blocks) - 1, [PTb, vb_], [otb])
                                for half, (ot, otb) in enumerate(((o0, o0b), (o1, o1b))):
                                    P.op("dve", lambda e, ot=ot, half=half: e.reciprocal(
                                        out=rinv[:, half * 4:(half + 1) * 4], in_=ot[:, :, 64]), [otb], [rinvb])
                                    P.tt("dve", nao[:, half * 256:(half + 1) * 256].rearrange("p (h d) -> p h d", h=4),
                                         ot[:, :, 0:64],
                                         rinv[:, half * 4:(half + 1) * 4].unsqueeze(2).to_broadcast([64, 4, 64]),
                                         ALU.mult, [otb, rinvb], [naob])
                                row = i * 128 + rr * 64
                                P.dma("pool", mixS[row:row + 64, 1536:2048], nao[:], reads=[naob], writes=[mixB[i]])

                        for i in out_tiles:
                            attend(i)

                mix_scope.__exit__(None, None, None)
                gbc, gbcb = P.sb("gbc", [128, 2, 2, 1024])
                P.dma("sp", gbc[:].rearrange("p a b c -> p (a b c)"), gS[:, :], reads=[gSb], writes=[gbcb])
                with P.scope():
                    wout, woutb = P.sb("wout", [128, 16, 1024], BF16)
                    for f in range(16):
                        P.dma("pool", wout[:, f, :], W["w_out"][l, f * 128:(f + 1) * 128, :], writes=[woutb])
                    tpA, tpAb = P.ps("tpA", [128, 8, 128], BF16)
                    outp, outpb = P.ps("outp", [128, 1024])
                    mt, mtb = P.sb("mt", [128, 2048], BF16)
                    mixT, mixTb = P.sb("mixT", [128, 16, 128], BF16)
                    h_t, h_tb = P.sb("h_t", [128, D])
                    hm, hmb = P.sb("hm", [128, D])
                    for i in out_tiles:
                        r = 1 if i < 2 else 0
                        P.dma("sp", mt[:], mixS[i * 128:(i + 1) * 128, :], reads=[mixB[i]], writes=[mtb])
                        if l == 0:
                            src = ctx_in[i * 128:(i + 1) * 128, :] if i < 2 else x_in[(i - 2) * 128:(i - 1) * 128, :]
                            rds = []
                        else:
                            src = hS[i * 128:(i + 1) * 128, :]
                            rds = [hB[i]]
                        P.dma("sp", h_t[:], src, reads=rds, writes=[h_tb])
                        if dbg and "mix" in dbg and l == 0:
                            for hh2 in range(2):
                                P.cp("dve", hm[:], mt[:, hh2 * 1024:(hh2 + 1) * 1024], [mtb], [hmb])
                                P.dma("sp", dbg_out["mix"][i * 128:(i + 1) * 128, hh2 * 1024:(hh2 + 1) * 1024], hm[:], reads=[hmb], writes=[dbgB])
                        for hf in range(2):
                            for f in range(8):
                                P.tr(tpA[:, f, :], mt[:, (hf * 8 + f) * 128:(hf * 8 + f + 1) * 128], identb[:], [mtb, identbb], [tpAb])
                            P.cp("act", mixT[:, hf * 8:(hf + 1) * 8, :], tpA[:], [tpAb], [mixTb])
                        for n in range(2):
                            for f in range(16):
                                P.mm(outp[:, n * 512:(n + 1) * 512], mixT[:, f, :], wout[:, f, n * 512:(n + 1) * 512],
                                     f == 0, f == 15, [mixTb, woutb], [outpb])
                        P.tt("dve", hm[:], outp[:], gbc[:, 0, r, :], ALU.mult, [outpb, gbcb], [hmb])
                        P.tt("dve", hm[:], hm[:], h_t[:], ALU.add, [hmb, h_tb], [hmb])
                        P.dma("sp", hS[i * 128:(i + 1) * 128, :], hm[:], reads=[hmb], writes=[hB[i]])
                        if dbg and "hmid" in dbg and l == 0:
                            P.dma("sp", dbg_out["hmid"][i * 128:(i + 1) * 128, :], hm[:], reads=[hmb], writes=[dbgB])
                if dbg and dbg.get("_stop") == "mix":
                    break
                with P.scope():
                    wq, wqb = P.sb("wq", [128, 8, 2048])
                    P.dma("sp", wq[:], W["peer_wq"][l].rearrange("(k p) n -> p k n", p=128), writes=[wqb])
                    keysT, keysTb = P.sb("keysT", [128, 16, 128])
                    fgbc, fgbcb = P.sb("fgbc", [128, D])
                    if last:
                        P.dma("sp", fgbc[:], W["final_g"].partition_broadcast(128), writes=[fgbcb])
                    outp, outpb = P.ps("outp", [128, 1024])
                    tpF, tpFb = P.ps("tpF", [128, 8, 128])
                    qps, qpsb = P.ps("qps", [128, 4, 128])
                    sps, spsb = P.ps("sps", [128, 4, 128])
                    with P.scope():
                        kraw, krawb = P.sb("kraw", [128, 16, 128])
                        P.dma("sp", kraw[:], W["peer_keys"][l].rearrange("h i k d -> k (h i) d"), writes=[krawb])
                        for j in range(16):
                            P.tr(tpF[:, j % 8, :], kraw[:, j, :], identf, [krawb, cstb], [tpFb])
                            if j % 8 == 7:
                                P.cp("dve", keysT[:, j - 7:j + 1, :], tpF[:], [tpFb], [keysTb])
                    hm, hmb = P.sb("hm", [128, D])
                    jk, jkb = P.sb("jk", [128, D])
                    s1, s1b = P.sb("s1", [128, 1])
                    xn2, xn2b = P.sb("xn2", [128, D])
                    u2T, u2Tb = P.sb("u2T", [128, 8, 128])
                    u2, u2b = P.sb("u2", [128, D])
                    big1, big1b = P.sb("big1", [128, 2048])
                    big2, big2b = P.sb("big2", [128, 2048])
                    qT = big1[:].rearrange("p (j t) -> p j t", j=16)
                    cidx = big1[:].rearrange("p (h c) -> p h c", h=8)
                    s_all = big2[:].rearrange("p (j t) -> p j t", j=16)
                    cand = big2[:].rearrange("p (h c) -> p h c", h=8)
                    qTb = cidxb = big1b
                    s_allb = candb = big2b
                    stmp, stmpb = P.sb("stmp", [128, 128])
                    v12, v12b = P.sb("v12", [128, 16, 16])
                    i12, i12b = P.sb("i12", [128, 16, 16], U32)
                    i12f, i12fb = P.sb("i12f", [128, 16, 16])
                    ctmp, ctmpb = P.sb("ctmp", [128, 256])
                    tops, topsb = P.sb("tops", [128, 8, 16])
                    pos, posb = P.sb("pos", [128, 8, 16], U32)
                    posf, posfb = P.sb("posf", [128, 8, 16])
                    eidf, eidfb = P.sb("eidf", [128, 128])
                    eid, eidb = P.sb("eid", [128, 128], I32)
                    nmax, nmaxb = P.sb("nmax", [128, 8])
                    eg, egb = P.sb("eg", [128, 8, 16])
                    esum, esumb = P.sb("esum", [128, 8])
                    gate, gateb = P.sb("gate", [128, 128])
                    actt, acttb = P.sb("actt", [128, 128])
                    wgt, wgtb = P.sb("wgt", [128, 128])
                    NB = 8
                    ub = [P.sb(f"ub{n}", [128, 2 * D], BF16) for n in range(NB)]
                    dg = [P.sb(f"dg{n}", [128, 128], BF16) for n in range(4)]
                    colA = [Buf(f"colA{n}") for n in range(128)]
                    colG = [Buf(f"colG{n}") for n in range(128)]
                    colE = [Buf(f"colE{n}") for n in range(128)]
                    ho, hob = P.sb("ho", [128, D])
                    iota = C("iota")

                    for i in out_tiles:
                        r = 1 if i < 2 else 0
                        P.dma("sp", hm[:], hS[i * 128:(i + 1) * 128, :], reads=[hB[i]], writes=[hmb])
                        P.actv(jk[:], hm[:], AF.Square, [hmb], [jkb, s1b], accum=s1[:])
                        P.actv(s1[:], s1[:], AF.Sqrt, [s1b], [s1b], bias=1e-6, scale=1.0 / D)
                        P.op("dve", lambda e: e.reciprocal(out=s1[:], in_=s1[:]), [s1b], [s1b])
                        P.ts("dve", xn2[:], hm[:], s1[:, 0:1], ALU.mult, [hmb, s1b], [xn2b])
                        for k in range(8):
                            P.tr(tpF[:, k, :], xn2[:, k * 128:(k + 1) * 128], identf, [xn2b, cstb], [tpFb])
                        for k in range(8):
                            P.actv(u2T[:, k, :], tpF[:, k, :], AF.Identity, [tpFb, A2b, modTb], [u2Tb],
                                   bias=B2[:, k, r:r + 1], scale=A2[:, k, r:r + 1])
                        for k in range(8):
                            P.tr(tpF[:, k, :], u2T[:, k, :], identf, [u2Tb, cstb], [tpFb])
                        P.cp("act", u2[:], tpF[:].rearrange("p a b -> p (a b)"), [tpFb], [u2b])
                        for j4 in range(4):
                            for jj in range(4):
                                j = j4 * 4 + jj
                                for k in range(8):
                                    P.mm(qps[:, jj, :], wq[:, k, j * 128:(j + 1) * 128], u2T[:, k, :], k == 0, k == 7,
                                         [wqb, u2Tb], [qpsb])
                            P.cp("dve", qT[:, j4 * 4:(j4 + 1) * 4, :], qps[:], [qpsb], [qTb])
                        for j4 in range(4):
                            for jj in range(4):
                                j = j4 * 4 + jj
                                P.mm(sps[:, jj, :], qT[:, j, :], keysT[:, j, :], True, True, [qTb, keysTb], [spsb])
                            P.cp("act", s_all[:, j4 * 4:(j4 + 1) * 4, :], sps[:], [spsb], [s_allb])
                        for j in range(16):
                            P.op("dve", lambda e, j=j: e.max(out=v12[:, j, 0:8], in_=s_all[:, j, :]), [s_allb], [v12b])
                            P.op("dve", lambda e, j=j: e.max_index(out=i12[:, j, 0:8], in_max=v12[:, j, 0:8], in_values=s_all[:, j, :]),
                                 [s_allb, v12b], [i12b])
                            P.op("dve", lambda e, j=j: e.match_replace(out=stmp[:], in_to_replace=v12[:, j, 0:8],
                                                                      in_values=s_all[:, j, :], imm_value=-1e30),
                                 [s_allb, v12b], [stmpb])
                            P.op("dve", lambda e, j=j: e.max(out=v12[:, j, 8:16], in_=stmp[:]), [stmpb], [v12b])
                            P.op("dve", lambda e, j=j: e.max_index(out=i12[:, j, 8:16], in_max=v12[:, j, 8:16], in_values=s_all[:, j, :]),
                                 [s_allb, v12b], [i12b])
                        P.cp("dve", i12f[:], i12[:], [i12b], [i12fb])
                        vv = v12[:].rearrange("p (h t) k -> p h t k", t=2)
                        ii = i12f[:].rearrange("p (h t) k -> p h t k", t=2)
                        P.ts("dve", ii[:, :, 0, :], ii[:, :, 0, :], 128.0, ALU.mult, [i12fb], [i12fb])
                        c4 = cand.rearrange("p h (a b) -> p h a b", a=16)
                        x4 = cidx.rearrange("p h (a b) -> p h a b", a=16)
                        P.tt("dve", c4, vv[:, :, 0, :].unsqueeze(3).to_broadcast([128, 8, 16, 16]),
                             vv[:, :, 1, :].unsqueeze(2).to_broadcast([128, 8, 16, 16]), ALU.add, [v12b], [candb])
                        P.tt("dve", x4, ii[:, :, 0, :].unsqueeze(3).to_broadcast([128, 8, 16, 16]),
                             ii[:, :, 1, :].unsqueeze(2).to_broadcast([128, 8, 16, 16]), ALU.add, [i12fb], [cidxb])
                        for h in range(8):
                            P.op("dve", lambda e, h=h: e.max(out=tops[:, h, 0:8], in_=cand[:, h, :]), [candb], [topsb])
                            P.op("dve", lambda e, h=h: e.max_index(out=pos[:, h, 0:8], in_max=tops[:, h, 0:8], in_values=cand[:, h, :]),
                                 [candb, topsb], [posb])
                            P.op("dve", lambda e, h=h: e.match_replace(out=ctmp[:], in_to_replace=tops[:, h, 0:8],
                                                                      in_values=cand[:, h, :], imm_value=-1e30),
                                 [candb, topsb], [ctmpb])
                            P.op("dve", lambda e, h=h: e.max(out=tops[:, h, 8:16], in_=ctmp[:]), [ctmpb], [topsb])
                            P.op("dve", lambda e, h=h: e.max_index(out=pos[:, h, 8:16], in_max=tops[:, h, 8:16], in_values=cand[:, h, :]),
                                 [candb, topsb], [posb])
                        P.cp("dve", posf[:], pos[:], [posb], [posfb])
                        for h in range(8):
                            for k in range(16):
                                P.stt(jk[:, 0:256], iota, posf[:, h, k:k + 1], cidx[:, h, :], ALU.is_equal, ALU.mult,
                                      [cstb, posfb, cidxb], [colE[h * 16 + k]], accum=eidf[:, h * 16 + k:h * 16 + k + 1])
                        P.ts("dve", eidf[:], eidf[:], 16383.0, ALU.min, colE + [eidfb], [eidfb], s2=0.0, op1=ALU.max)
                        if l > 0:
                            P.ts("dve", eidf[:], eidf[:], float(l * 16384), ALU.add, [eidfb], [eidfb])
                        P.cp("dve", eid[:], eidf[:], [eidfb], [eidb])
                        P.ts("dve", nmax[:], tops[:, :, 0], -1.0, ALU.mult, [topsb], [nmaxb])
                        for h in range(8):
                            P.actv(eg[:, h, :], tops[:, h, :], AF.Exp, [topsb, nmaxb], [egb, esumb], bias=nmax[:, h:h + 1],
                                   accum=esum[:, h:h + 1])
                        P.op("dve", lambda e: e.reciprocal(out=esum[:], in_=esum[:]), [esumb], [esumb])
                        P.tt("dve", gate[:].rearrange("p (h k) -> p h k", h=8), eg[:],
                             esum[:].unsqueeze(2).to_broadcast([128, 8, 16]), ALU.mult, [egb, esumb], [gateb])
                        for hk in range(128):
                            ut, utb = ub[hk % NB]
                            dgt, dgb = dg[hk % 4]
                            P.idma(ut[:], UV, eid[:, hk:hk + 1], reads=[eidb, UVb], writes=[utb])
                            P.stt(jk[:], ut[:, 0:1024], 1.0, u2[:], ALU.mult, ALU.mult, [utb, u2b], [colA[hk]],
                                  accum=actt[:, hk:hk + 1])
                            utb.rd.pop("dve", None)
                            P.actv(wgt[:, hk:hk + 1], actt[:, hk:hk + 1], AF.Gelu, [colA[hk]], [colG[hk]])
                            P.ts("dve", dgt[:], identf, wgt[:, hk:hk + 1], ALU.mult, [cstb, colG[hk], gateb], [dgb],
                                 s2=gate[:, hk:hk + 1], op1=ALU.mult)
                            for n in range(2):
                                P.mm(outp[:, n * 512:(n + 1) * 512], dgt[:], ut[:, 1024 + n * 512:1024 + (n + 1) * 512],
                                     hk == 0, hk == 127, [dgb, utb], [outpb])
                        if dbg and "peer" in dbg and l == 0:
                            P.cp("act", ho[:], outp[:], [outpb], [hob])
                            P.dma("sp", dbg_out["peer"][i * 128:(i + 1) * 128, :], ho[:], reads=[hob], writes=[dbgB])
                        P.tt("dve", ho[:], outp[:], gbc[:, 1, r, :], ALU.mult, [outpb, gbcb], [hob])
                        P.tt("dve", ho[:], ho[:], hm[:], ALU.add, [hob, hmb], [hob])
                        if dbg and "h0" in dbg and l == 0:
                            P.dma("sp", dbg_out["h0"][i * 128:(i + 1) * 128, :], ho[:], reads=[hob], writes=[dbgB])
                        if not last:
                            P.dma("sp", hS[i * 128:(i + 1) * 128, :], ho[:], reads=[hob], writes=[hB[i]])
                        else:
                            P.actv(jk[:], ho[:], AF.Square, [hob], [jkb, s1b], accum=s1[:])
                            P.actv(s1[:], s1[:], AF.Sqrt, [s1b], [s1b], bias=1e-6, scale=1.0 / D)
                            P.op("dve", lambda e: e.reciprocal(out=s1[:], in_=s1[:]), [s1b], [s1b])
                            P.stt(xn2[:], ho[:], s1[:, 0:1], fgbc[:], ALU.mult, ALU.mult, [hob, s1b, fgbcb], [xn2b])
                            P.dma("sp", y_out[(i - 2) * 128:(i - 1) * 128, :], xn2[:], reads=[xn2b], writes=[yB])
        P.barrier()
        print("program: instr", P.ninstr, "waits", P.nwaits, "per-engine", P.cnt)
    return nc


_NC_CACHE = {}


def kernel(**inputs):
    n = 4
    if "nc" not in _NC_CACHE:
        _NC_CACHE["nc"] = build_program()
    nc = _NC_CACHE["nc"]
    shared = {name: np.ascontiguousarray(inputs[name], dtype=np.float32) for name, _ in WSPEC}
    shared["cst"] = CST
    in_maps = []
    for b in range(n):
        m = dict(shared)
        m["x"] = np.ascontiguousarray(inputs["x"][b], dtype=np.float32)
        m["ctx"] = np.ascontiguousarray(inputs["ctx"][b], dtype=np.float32)
        m["c2"] = np.ascontiguousarray(np.stack([inputs["c"][b], inputs["c_ctx"]]), dtype=np.float32)
        in_maps.append(m)
    res = run_bass_kernel_spmd(nc, in_maps, core_ids=list(range(n)))
    return np.stack([res.results[b]["y"] for b in range(n)], axis=0).astype(np.float32)
```
